# Optimizing a Trainium2 kernel written in Bass

```python
import math
import jax
import jax.numpy as jnp
from jax import lax
import numpy as np

D_MODEL = 1024
BATCH = 4
SEQ = 4096
DEPTH = 2

GRID_W = 64
CTX_LEN = 256
EPS = 1e-6
ROPE_BASE = 10000.0
Q_BLOCK = 128

MLA_HEADS = 8
MLA_Q_RANK = 256
MLA_KV_RANK = 256
MLA_NOPE = 64
MLA_ROPE = 32
MLA_V = 64
MLA_SCALE = (MLA_NOPE + MLA_ROPE) ** -0.5

DIFF_HEADS = 4
DIFF_DH = 64

SWA_HEADS = 8
SWA_KV_HEADS = 2
SWA_DH = 64
WINDOW = 128
SWA_BLOCK = WINDOW

D_FF = -(-8 * D_MODEL // (3 * 256)) * 256

N_BRANCH = 3
BRANCH_A = MLA_HEADS * MLA_V
BRANCH_B = DIFF_HEADS * 2 * DIFF_DH
BRANCH_C = SWA_HEADS * SWA_DH
KV_SPLITS = (MLA_KV_RANK, MLA_ROPE, 2 * DIFF_HEADS * DIFF_DH, DIFF_HEADS * 2 * DIFF_DH,
             SWA_KV_HEADS * SWA_DH, SWA_KV_HEADS * SWA_DH)
Q_SPLITS = (MLA_Q_RANK, 2 * DIFF_HEADS * DIFF_DH, SWA_HEADS * SWA_DH, N_BRANCH * D_MODEL)
KV_WIDTH = sum(KV_SPLITS)
IN_WIDTH = KV_WIDTH + sum(Q_SPLITS)

kernel_name = 'hybrid_mla_diff_swa_prefix_dit'


def rms_norm(x, w):
    xf = x.astype(jnp.float32)
    y = xf * lax.rsqrt(jnp.mean(xf * xf, axis=-1, keepdims=True) + EPS)
    return (y * w.astype(jnp.float32)).astype(x.dtype)


def split_cols(z, widths):
    return jnp.split(z, np.cumsum(widths)[:-1].tolist(), axis=-1)


def axial_rope_tables(rows, rot_dim):
    r = jnp.repeat(jnp.arange(rows, dtype=jnp.float32), GRID_W)
    col = jnp.tile(jnp.arange(GRID_W, dtype=jnp.float32), rows)
    axis_dim = rot_dim // 2
    inv = ROPE_BASE ** (-jnp.arange(0, axis_dim, 2, dtype=jnp.float32) / axis_dim)
    ang_r = r[:, None] * inv
    ang_c = col[:, None] * inv
    return (jnp.cos(ang_r), jnp.sin(ang_r), jnp.cos(ang_c), jnp.sin(ang_c))


def rope_1d(x, cos, sin):
    x1, x2 = jnp.split(x, 2, axis=-1)
    c = cos[None, :, None, :]
    s = sin[None, :, None, :]
    return jnp.concatenate([x1 * c - x2 * s, x2 * c + x1 * s], axis=-1)


def rope_2d(x, tab):
    cr, sr, cc, sc = tab
    d = x.shape[-1]
    out = jnp.concatenate([rope_1d(x[..., : d // 2], cr, sr), rope_1d(x[..., d // 2:], cc, sc)], axis=-1)
    return out.astype(x.dtype)


def adaln(cond, w, b):
    return jnp.split(jax.nn.silu(cond) @ w + b, 6, axis=-1)


def modulate(h, shift, scale):
    return h * (1.0 + scale) + shift


def project_kv(h, w_kv, kv_norm, w_ukv, tabs):
    B, L, _ = h.shape
    ckv, kr, dk, dv, sk, sv = split_cols(h @ w_kv, KV_SPLITS)
    kv = (rms_norm(ckv, kv_norm) @ w_ukv).reshape(B, L, MLA_HEADS, MLA_NOPE + MLA_V)
    kr = kr.reshape(B, L, 1, MLA_ROPE)
    dk = dk.reshape(B, L, 2 * DIFF_HEADS, DIFF_DH)
    sk = sk.reshape(B, L, SWA_KV_HEADS, SWA_DH)
    if tabs is not None:
        kr = rope_2d(kr, tabs[MLA_ROPE])
        dk = rope_2d(dk, tabs[DIFF_DH])
        sk = rope_2d(sk, tabs[SWA_DH])
    return {
        'mla_k': kv[..., :MLA_NOPE], 'mla_v': kv[..., MLA_NOPE:], 'mla_kr': kr[:, :, 0],
        'diff_k': dk, 'diff_v': dv.reshape(B, L, DIFF_HEADS, 2 * DIFF_DH),
        'swa_k': sk, 'swa_v': sv.reshape(B, L, SWA_KV_HEADS, SWA_DH),
    }


def project_q(h, w_q, q_norm, w_uq, tabs):
    B, L, _ = h.shape
    cq, dq, sq, gates = split_cols(h @ w_q, Q_SPLITS)
    q = (rms_norm(cq, q_norm) @ w_uq).reshape(B, L, MLA_HEADS, MLA_NOPE + MLA_ROPE)
    qn, qr = q[..., :MLA_NOPE], q[..., MLA_NOPE:]
    dq = dq.reshape(B, L, 2 * DIFF_HEADS, DIFF_DH)
    sq = sq.reshape(B, L, SWA_HEADS, SWA_DH)
    if tabs is not None:
        qr = rope_2d(qr, tabs[MLA_ROPE])
        dq = rope_2d(dq, tabs[DIFF_DH])
        sq = rope_2d(sq, tabs[SWA_DH])
    return {'mla_qn': qn, 'mla_qr': qr, 'diff_q': dq, 'swa_q': sq, 'gates': jax.nn.sigmoid(gates)}


def sweep_query_blocks(fn, *qs):
    B, S = qs[0].shape[:2]
    nb = S // Q_BLOCK
    blocks = tuple(jnp.moveaxis(q.reshape(B, nb, Q_BLOCK, *q.shape[2:]), 1, 0) for q in qs)
    out = lax.map(lambda blk: fn(*blk), blocks)
    return jnp.moveaxis(out, 0, 1).reshape(B, S, *out.shape[3:])


def mla_attend(qn, qr, kn, kr, v):
    s = (jnp.einsum('bqhd,bkhd->bhqk', qn, kn) + jnp.einsum('bqhr,bkr->bhqk', qr, kr)).astype(jnp.float32)
    p = jax.nn.softmax(s * MLA_SCALE, axis=-1).astype(v.dtype)
    return jnp.einsum('bhqk,bkhd->bqhd', p, v)


def diff_attend(q, k, v, lam):
    B, Lq, H2, dh = q.shape
    s = jnp.einsum('bqhd,bkhd->bhqk', q, k).astype(jnp.float32) * (dh ** -0.5)
    p = jax.nn.softmax(s, axis=-1).reshape(B, H2 // 2, 2, Lq, -1)
    a = (p[:, :, 0] - lam * p[:, :, 1]).astype(v.dtype)
    return jnp.einsum('bhqk,bkhd->bqhd', a, v)


def diff_finish(o, norm_w, lam_init):
    B, L = o.shape[:2]
    return (rms_norm(o, norm_w) * (1.0 - lam_init)).reshape(B, L, BRANCH_B)


def swa_latent(q, k, v, kc, vc, sink):
    B, S, Hq, dh = q.shape
    Hkv = k.shape[2]
    G = Hq // Hkv
    nb = S // SWA_BLOCK
    qb = q.reshape(B, nb, SWA_BLOCK, Hkv, G, dh)

    def band(t):
        tp = jnp.pad(t, ((0, 0), (SWA_BLOCK, SWA_BLOCK), (0, 0), (0, 0)))
        tp = tp.reshape(B, nb + 2, SWA_BLOCK, *t.shape[2:])
        return jnp.concatenate([tp[:, :-2], tp[:, 1:-1], tp[:, 2:]], axis=2)

    kb, vb = band(k), band(v)
    scale = dh ** -0.5
    s_loc = jnp.einsum('bnqhgd,bnkhd->bnhgqk', qb, kb).astype(jnp.float32) * scale
    qpos = jnp.arange(S).reshape(nb, SWA_BLOCK)
    kpos = (jnp.arange(nb)[:, None] - 1) * SWA_BLOCK + jnp.arange(3 * SWA_BLOCK)[None, :]
    kp = kpos[:, None, :]
    mask = (jnp.abs(qpos[:, :, None] - kp) <= WINDOW) & (kp >= 0) & (kp < S)
    s_loc = jnp.where(mask[None, :, None, None], s_loc, -jnp.inf)
    s_ctx = jnp.einsum('bnqhgd,bkhd->bnhgqk', qb, kc).astype(jnp.float32) * scale
    sink_col = jnp.broadcast_to(sink.reshape(Hkv, G)[None, None, :, :, None, None].astype(jnp.float32),
                                s_loc.shape[:-1] + (1,))
    p = jax.nn.softmax(jnp.concatenate([s_loc, s_ctx, sink_col], axis=-1), axis=-1).astype(v.dtype)
    nloc = 3 * SWA_BLOCK
    nctx = kc.shape[1]
    o = (jnp.einsum('bnhgqk,bnkhd->bnqhgd', p[..., :nloc], vb)
         + jnp.einsum('bnhgqk,bkhd->bnqhgd', p[..., nloc:nloc + nctx], vc))
    return o.reshape(B, S, Hq * dh)


def swa_context(q, kc, vc, sink):
    B, C, Hq, dh = q.shape
    Hkv = kc.shape[2]
    G = Hq // Hkv
    qg = q.reshape(B, C, Hkv, G, dh)
    s = jnp.einsum('bqhgd,bkhd->bhgqk', qg, kc).astype(jnp.float32) * (dh ** -0.5)
    sink_col = jnp.broadcast_to(sink.reshape(Hkv, G)[None, :, :, None, None].astype(jnp.float32),
                                s.shape[:-1] + (1,))
    p = jax.nn.softmax(jnp.concatenate([s, sink_col], axis=-1), axis=-1)[..., :-1].astype(vc.dtype)
    return jnp.einsum('bhgqk,bkhd->bqhgd', p, vc).reshape(B, C, Hq * dh)


def latent_branches(qx, kx, kc, lam, lam_init, diff_norm_w, sink):
    B, S = qx['swa_q'].shape[:2]
    cat = lambda n: jnp.concatenate([kc[n], kx[n]], axis=1)
    mk, mkr, mv, dk, dv = cat('mla_k'), cat('mla_kr'), cat('mla_v'), cat('diff_k'), cat('diff_v')
    o_a = sweep_query_blocks(lambda qn, qr: mla_attend(qn, qr, mk, mkr, mv), qx['mla_qn'], qx['mla_qr'])
    o_b = sweep_query_blocks(lambda q: diff_attend(q, dk, dv, lam), qx['diff_q'])
    o_c = swa_latent(qx['swa_q'], kx['swa_k'], kx['swa_v'], kc['swa_k'], kc['swa_v'], sink)
    return o_a.reshape(B, S, BRANCH_A), diff_finish(o_b, diff_norm_w, lam_init), o_c


def context_branches(qc, kc, lam, lam_init, diff_norm_w, sink):
    B, C = qc['swa_q'].shape[:2]
    o_a = mla_attend(qc['mla_qn'], qc['mla_qr'], kc['mla_k'], kc['mla_kr'], kc['mla_v'])
    o_b = diff_attend(qc['diff_q'], kc['diff_k'], kc['diff_v'], lam)
    o_c = swa_context(qc['swa_q'], kc['swa_k'], kc['swa_v'], sink)
    return o_a.reshape(B, C, BRANCH_A), diff_finish(o_b, diff_norm_w, lam_init), o_c


def merge_branches(o_a, o_b, o_c, gates, wa, wb, wc, wo):
    ga, gb, gc = jnp.split(gates, N_BRANCH, axis=-1)
    return (ga * (o_a @ wa) + gb * (o_b @ wb) + gc * (o_c @ wc)) @ wo


def swiglu(h, w_gate_up, w_down):
    g, u = jnp.split(h @ w_gate_up, 2, axis=-1)
    return (jax.nn.silu(g) * u) @ w_down


def setup_inputs(seed: int = 0) -> dict:
    key = jax.random.key(seed)
    ks = iter(jax.random.split(key, 32))
    L, D = DEPTH, D_MODEL

    def nrm(shape, std=1.0):
        return jax.random.normal(next(ks), shape, jnp.float32) * std

    def w(shape, fan_in, gain=1.0):
        return nrm(shape, gain * fan_in ** -0.5)

    def gain_vec(shape):
        return 1.0 + nrm(shape, 0.05)

    return {
        'x': nrm((BATCH, SEQ, D)),
        'c': nrm((BATCH, D)),
        'ctx': nrm((BATCH, CTX_LEN, D)),
        'c_ctx': nrm((D,)),
        'w_ada': w((L, D, 6 * D), D, 0.5),
        'b_ada': nrm((L, 6 * D), 0.05),
        'attn_pre_norm': gain_vec((L, D)),
        'attn_post_norm': gain_vec((L, D)),
        'ffn_pre_norm': gain_vec((L, D)),
        'ffn_post_norm': gain_vec((L, D)),
        'w_in': w((L, D, IN_WIDTH), D),
        'mla_q_norm': gain_vec((L, MLA_Q_RANK)),
        'w_uq': w((L, MLA_Q_RANK, MLA_HEADS * (MLA_NOPE + MLA_ROPE)), MLA_Q_RANK),
        'mla_kv_norm': gain_vec((L, MLA_KV_RANK)),
        'w_ukv': w((L, MLA_KV_RANK, MLA_HEADS * (MLA_NOPE + MLA_V)), MLA_KV_RANK),
        'diff_lambda': nrm((L, 4, DIFF_DH), 0.1),
        'diff_norm': gain_vec((L, 2 * DIFF_DH)),
        'swa_sink': nrm((L, SWA_HEADS), 0.5),
        'w_branch_a': w((L, BRANCH_A, D), BRANCH_A),
        'w_branch_b': w((L, BRANCH_B, D), BRANCH_B),
        'w_branch_c': w((L, BRANCH_C, D), BRANCH_C),
        'w_o': w((L, D, D), D),
        'w_gate_up': w((L, D, 2 * D_FF), D),
        'w_down': w((L, D_FF, D), D_FF),
    }


def reference(x, c, ctx, c_ctx, w_ada, b_ada, attn_pre_norm, attn_post_norm, ffn_pre_norm,
              ffn_post_norm, w_in, mla_q_norm, w_uq, mla_kv_norm, w_ukv, diff_lambda, diff_norm,
              swa_sink, w_branch_a, w_branch_b, w_branch_c, w_o, w_gate_up, w_down):
    S = x.shape[1]
    rows = S // GRID_W
    tabs = {d: axial_rope_tables(rows, d) for d in sorted({MLA_ROPE, DIFF_DH, SWA_DH})}
    for l in range(DEPTH):
        last = l == DEPTH - 1
        lam_init = 0.8 - 0.6 * math.exp(-0.3 * l)
        lq1, lk1, lq2, lk2 = (diff_lambda[l, i].astype(jnp.float32) for i in range(4))
        lam = jnp.exp(jnp.sum(lq1 * lk1)) - jnp.exp(jnp.sum(lq2 * lk2)) + lam_init
        mx = adaln(c[:, None, :], w_ada[l], b_ada[l])
        mc = adaln(c_ctx[None, None, :], w_ada[l], b_ada[l])
        w_kv, w_q = w_in[l][:, :KV_WIDTH], w_in[l][:, KV_WIDTH:]

        hx = modulate(rms_norm(x, attn_pre_norm[l]), mx[0], mx[1])
        hc = modulate(rms_norm(ctx, attn_pre_norm[l]), mc[0], mc[1])
        kx = project_kv(hx, w_kv, mla_kv_norm[l], w_ukv[l], tabs)
        kc = project_kv(hc, w_kv, mla_kv_norm[l], w_ukv[l], None)
        qx = project_q(hx, w_q, mla_q_norm[l], w_uq[l], tabs)
        o_a, o_b, o_c = latent_branches(qx, kx, kc, lam, lam_init, diff_norm[l], swa_sink[l])
        y = merge_branches(o_a, o_b, o_c, qx['gates'], w_branch_a[l], w_branch_b[l], w_branch_c[l], w_o[l])
        x = x + mx[2] * rms_norm(y, attn_post_norm[l])
        if not last:
            qc = project_q(hc, w_q, mla_q_norm[l], w_uq[l], None)
            p_a, p_b, p_c = context_branches(qc, kc, lam, lam_init, diff_norm[l], swa_sink[l])
            yc = merge_branches(p_a, p_b, p_c, qc['gates'], w_branch_a[l], w_branch_b[l], w_branch_c[l], w_o[l])
            ctx = ctx + mc[2] * rms_norm(yc, attn_post_norm[l])

        fx = swiglu(modulate(rms_norm(x, ffn_pre_norm[l]), mx[3], mx[4]), w_gate_up[l], w_down[l])
        x = x + mx[5] * rms_norm(fx, ffn_post_norm[l])
        if not last:
            fc = swiglu(modulate(rms_norm(ctx, ffn_pre_norm[l]), mc[3], mc[4]), w_gate_up[l], w_down[l])
            ctx = ctx + mc[5] * rms_norm(fc, ffn_post_norm[l])
    return x
```

```python
from contextlib import ExitStack
import math
import numpy as np
import concourse.bass as bass
import concourse.mybir as mybir
from concourse.bass_utils import run_bass_kernel_spmd

F32 = mybir.dt.float32
BF16 = mybir.dt.bfloat16
AF = mybir.ActivationFunctionType
ALU = mybir.AluOpType
AX = mybir.AxisListType

D = 1024
KC = 8
NCTX = 256
SOWN = 2048
TA = NCTX + SOWN
TB = 2048
NK = TA + TB
NKT = NK // 128
EPS = 1e-6
MLA_SCALE = 96 ** -0.5
DH_SCALE = 0.125
NCH = 61
DFF = 2816
NFC = 22
DEPTH = 2

CH_CKV = (0, 1)
CH_KR, CH_KRS = 2, 3
CH_DK = (4, 5, 6, 7)
CH_DKS = (8, 9, 10, 11)
CH_DV = (12, 13, 14, 15)
CH_SK, CH_SKS, CH_SV = 16, 17, 18
CH_CQ = (19, 20)
CH_DQ = (21, 22, 23, 24)
CH_DQS = (25, 26, 27, 28)
CH_SQ = (29, 30, 31, 32)
CH_SQS = (33, 34, 35, 36)
CH_G0 = 37


class Sched:
    SEM_LIMIT = 30000
    NDMA = 10

    def __init__(self, nc, stack):
        self.nc = nc
        self.stack = stack
        self.eng = {"pe": nc.tensor, "act": nc.scalar, "dve": nc.vector,
                    "pool": nc.gpsimd, "sp": nc.sync}
        self.nsem = 0
        self.cur = {}
        for e in ("pe", "act", "dve", "pool"):
            self.cur[e] = [self._newsem(e), 0]
        self.dq = {q: {"sems": [], "vals": [], "i": 0} for q in ("sp", "pool", "act")}
        self.waited = {}
        self.state = {}
        self.persist = set()
        self.n_wait = 0
        self.n_ins = 0

    def _newsem(self, name):
        self.nsem += 1
        return self.stack.enter_context(self.nc.semaphore(f"s_{name}_{self.nsem}"))

    def _wait(self, e, deps):
        best = {}
        for (s, v) in deps:
            k = id(s)
            if k not in best or best[k][1] < v:
                best[k] = (s, v)
        for k, (s, v) in best.items():
            if e == "pe" and s is self.cur["pe"][0]:
                continue
            if self.waited.get((e, k), 0) >= v:
                continue
            self.eng[e].wait_ge(s, v)
            self.waited[(e, k)] = v
            self.n_wait += 1

    def _deps(self, reads, writes):
        deps = []
        for k in reads:
            st = self.state.get(k)
            if st is not None and st[0] is not None:
                deps.append(st[0])
        for k in writes:
            st = self.state.get(k)
            if st is not None:
                if st[0] is not None:
                    deps.append(st[0])
                deps.extend(st[1].values())
        return deps

    def _update(self, tok, reads, writes):
        sid = id(tok[0])
        for k in reads:
            st = self.state.setdefault(k, [None, {}])
            old = st[1].get(sid)
            if old is None or old[1] < tok[1]:
                st[1][sid] = tok
        for k in writes:
            self.state[k] = [tok, {}]

    def op(self, e, fn, reads=(), writes=()):
        self._wait(e, self._deps(reads, writes))
        ins = fn()
        c = self.cur[e]
        c[1] += 1
        ins.then_inc(c[0], 1)
        tok = (c[0], c[1])
        if c[1] >= self.SEM_LIMIT:
            self.cur[e] = [self._newsem(e), 0]
        self._update(tok, reads, writes)
        self.n_ins += 1
        return tok

    def dma(self, q, out, in_, reads=(), writes=()):
        d = self.dq[q]
        if len(d["sems"]) < self.NDMA:
            d["sems"].append(self._newsem("dma" + q))
            d["vals"].append(0)
        i = d["i"] % self.NDMA
        d["i"] += 1
        if d["vals"][i] + 16 > self.SEM_LIMIT:
            d["sems"][i] = self._newsem("dma" + q)
            d["vals"][i] = 0
        s, v = d["sems"][i], d["vals"][i]
        deps = self._deps(reads, writes)
        if v > 0:
            deps.append((s, v))
        self._wait(q, deps)
        self.eng[q].dma_start(out=out, in_=in_).then_inc(s, 16)
        d["vals"][i] = v + 16
        tok = (s, v + 16)
        self._update(tok, reads, writes)
        self.n_ins += 1
        return tok

    def all_tokens(self):
        toks = []
        for e, c in self.cur.items():
            if c[1] > 0:
                toks.append((c[0], c[1]))
        for q, d in self.dq.items():
            for s, v in zip(d["sems"], d["vals"]):
                if v > 0:
                    toks.append((s, v))
        return toks

    def barrier(self, engines=("pe", "act", "dve", "pool", "sp")):
        toks = self.all_tokens()
        for e in engines:
            self._wait(e, toks)
        self.state = {k: v for k, v in self.state.items() if k in self.persist}

    def collective(self, fn, reads=(), writes=()):
        self._wait("pool", self._deps(reads, writes))
        sem = self._newsem("cc")
        fn().then_inc(sem, 1)
        tok = (sem, 1)
        self._update(tok, reads, writes)
        for k in writes:
            self.persist.add(k)
        return tok


def _fm(a):
    T = a.shape[0]
    return np.ascontiguousarray(a.T.reshape(KC, 128, T).transpose(1, 0, 2))


def _partner(d):
    q = d // 4
    idx = np.arange(d)
    return np.where((idx % (2 * q)) < q, idx + q, idx - q)


def _swap_heads(cols, d):
    n = cols.shape[1] // d
    p = _partner(d)
    full = np.concatenate([h * d + p for h in range(n)])
    return cols[:, full]


def _chunk(cols):
    K, w = cols.shape
    out = np.zeros((K, 128), np.float32)
    out[:, :w] = cols
    return out.reshape(K // 128, 128, 128).transpose(1, 0, 2)


def _rope_tables(pos, d):
    n = pos.shape[0]
    q = d // 4
    axis_dim = d // 2
    inv = (10000.0 ** (-np.arange(0, axis_dim, 2, dtype=np.float32) / axis_dim)).astype(np.float32)
    valid = pos >= 0
    p = np.where(valid, pos, 0)
    r = (p // 64).astype(np.float32)
    c = (p % 64).astype(np.float32)
    ang_r = r[None, :] * inv[:, None]
    ang_c = c[None, :] * inv[:, None]
    C = np.zeros((d, n), np.float32)
    S = np.zeros((d, n), np.float32)
    C[0:q] = np.cos(ang_r); C[q:2 * q] = np.cos(ang_r)
    C[2 * q:3 * q] = np.cos(ang_c); C[3 * q:] = np.cos(ang_c)
    S[0:q] = -np.sin(ang_r); S[q:2 * q] = np.sin(ang_r)
    S[2 * q:3 * q] = -np.sin(ang_c); S[3 * q:] = np.sin(ang_c)
    C[:, ~valid] = 1.0
    S[:, ~valid] = 0.0
    return C, S


def _prep_weights(inp):
    L = DEPTH
    w = {}
    w_ada = np.asarray(inp["w_ada"], np.float32)
    w["w_ada"] = np.ascontiguousarray(
        w_ada.reshape(L, KC, 128, 12, 512).transpose(0, 3, 2, 1, 4))
    w["b_ada"] = np.ascontiguousarray(
        np.asarray(inp["b_ada"], np.float32).reshape(L, 48, 128).transpose(0, 2, 1))
    nws = np.stack([np.asarray(inp[k], np.float32) for k in
                    ("attn_pre_norm", "attn_post_norm", "ffn_pre_norm", "ffn_post_norm")], axis=1)
    w["nw"] = np.ascontiguousarray(nws.reshape(L, 4, KC, 128).transpose(0, 3, 1, 2))
    w_in = np.asarray(inp["w_in"], np.float32)
    chunks_all = []
    for l in range(L):
        W = w_in[l]
        kv, qq = W[:, :1568], W[:, 1568:]
        ch = []
        ch += [kv[:, 0:128], kv[:, 128:256]]
        kr = kv[:, 256:288]
        ch += [kr, _swap_heads(kr, 32)]
        dk = kv[:, 288:800]
        dks = _swap_heads(dk, 64)
        ch += [dk[:, i * 128:(i + 1) * 128] for i in range(4)]
        ch += [dks[:, i * 128:(i + 1) * 128] for i in range(4)]
        dv = kv[:, 800:1312]
        ch += [dv[:, i * 128:(i + 1) * 128] for i in range(4)]
        sk = kv[:, 1312:1440]
        ch += [sk, _swap_heads(sk, 64), kv[:, 1440:1568]]
        ch += [qq[:, 0:128], qq[:, 128:256]]
        dq = qq[:, 256:768]
        dqs = _swap_heads(dq, 64)
        ch += [dq[:, i * 128:(i + 1) * 128] for i in range(4)]
        ch += [dqs[:, i * 128:(i + 1) * 128] for i in range(4)]
        sq = qq[:, 768:1280]
        sqs = _swap_heads(sq, 64)
        ch += [sq[:, i * 128:(i + 1) * 128] for i in range(4)]
        ch += [sqs[:, i * 128:(i + 1) * 128] for i in range(4)]
        g = qq[:, 1280:]
        ch += [g[:, i * 128:(i + 1) * 128] for i in range(24)]
        assert len(ch) == NCH
        chunks_all.append(np.stack([_chunk(c) for c in ch]))
    w["w_in"] = np.ascontiguousarray(np.stack(chunks_all))
    w_ukv = np.asarray(inp["w_ukv"], np.float32).reshape(L, 2, 128, 8, 128)
    nope = w_ukv[..., 0:64].reshape(L, 2, 128, 4, 128)
    w["w_ukn"] = np.ascontiguousarray(nope.transpose(0, 2, 3, 1, 4))
    vv = w_ukv[..., 64:128].reshape(L, 2, 128, 512)
    w["w_ukv"] = np.ascontiguousarray(vv.transpose(0, 2, 1, 3))
    w_uq = np.asarray(inp["w_uq"], np.float32)
    uq = w_uq.reshape(L, 2, 128, 8, 96)
    w["w_uq"] = np.ascontiguousarray(uq.transpose(0, 2, 3, 1, 4))
    uqs = uq.copy()
    uqs[..., 64:96] = uq[..., 64:96][..., _partner(32)]
    w["w_uqs"] = np.ascontiguousarray(uqs.transpose(0, 2, 3, 1, 4))
    w["qn"] = np.ascontiguousarray(np.asarray(inp["mla_q_norm"], np.float32).reshape(L, 2, 128).transpose(0, 2, 1))
    w["kvn"] = np.ascontiguousarray(np.asarray(inp["mla_kv_norm"], np.float32).reshape(L, 2, 128).transpose(0, 2, 1))
    w["dlam"] = np.ascontiguousarray(np.asarray(inp["diff_lambda"], np.float32).reshape(L, 1, 256))
    w["dnorm"] = np.ascontiguousarray(np.asarray(inp["diff_norm"], np.float32).reshape(L, 2, 64).transpose(0, 2, 1))
    w["sink"] = np.ascontiguousarray(np.asarray(inp["swa_sink"], np.float32).reshape(L, 1, 8))
    br = np.stack([np.asarray(inp[k], np.float32) for k in ("w_branch_a", "w_branch_b", "w_branch_c")], axis=1)
    w["w_br"] = np.ascontiguousarray(br.reshape(L, 3, 4, 128, 1024).transpose(0, 1, 3, 2, 4))
    w["w_o"] = np.ascontiguousarray(
        np.asarray(inp["w_o"], np.float32).reshape(L, KC, 128, 1024).transpose(0, 2, 1, 3))
    gu = np.asarray(inp["w_gate_up"], np.float32)
    g_ = gu[:, :, :DFF].reshape(L, KC, 128, NFC, 128)
    u_ = gu[:, :, DFF:].reshape(L, KC, 128, NFC, 128)
    w["w_gu"] = np.ascontiguousarray(np.concatenate([g_, u_], axis=-1).transpose(0, 3, 2, 1, 4))
    dn = np.asarray(inp["w_down"], np.float32).reshape(L, NFC, 128, KC, 128)
    w["w_dn"] = np.ascontiguousarray(dn.transpose(0, 3, 2, 1, 4))
    w["ident"] = np.eye(128, dtype=np.float32)
    return w


def _prep_core_consts(half):
    own = np.arange(half * SOWN, (half + 1) * SOWN)
    oth = np.arange((1 - half) * SOWN, (2 - half) * SOWN)
    pos = np.concatenate([-np.ones(NCTX, np.int64), own, oth])
    C64, S64 = _rope_tables(pos, 64)
    C32, S32 = _rope_tables(pos, 32)
    t64 = np.stack([np.concatenate([C64, C64]), np.concatenate([S64, S64])])
    z = np.zeros((32, NK), np.float32)
    t32 = np.stack([np.concatenate([C32, z, C32, z]), np.concatenate([S32, z, S32, z])])
    kk = np.arange(128)[:, None]
    qq = np.arange(128)[None, :]
    NEG = -30000.0
    prev = np.where(qq <= kk, 0.0, NEG).astype(np.float32)
    nxt = np.where(kk <= qq, 0.0, NEG).astype(np.float32)
    full = np.full((128, 128), NEG, np.float32)
    m = np.stack([full if half == 0 else prev, prev, nxt, nxt if half == 0 else full], axis=1)
    m = np.ascontiguousarray(np.repeat(m[:, :, None, :], 4, axis=2).reshape(128, 4, 512))
    return {"rope64": np.ascontiguousarray(t64), "rope32": np.ascontiguousarray(t32), "maskb": m}


class Prog:
    def __init__(self, layers, dbg=False, stop=None, stop_layer=None):
        self.stop_cfg = stop
        self.stop_layer = stop_layer
        self.stop = stop
        self.layers = layers
        self.dbg = dbg
        self.nc = bass.Bass("TRN2", target_bir_lowering=False)
        self.uid = 0

    def din(self, name, shape):
        return self.nc.dram_tensor(name, list(shape), F32, kind="ExternalInput").ap()

    def dscr(self, name, shape, dt=BF16):
        kind = "ExternalOutput" if self.dbg else "Internal"
        return self.nc.dram_tensor(name, list(shape), dt, kind=kind).ap()

    def sb(self, stack, name, shape, dt):
        self.uid += 1
        return stack.enter_context(self.nc.sbuf_tensor(f"{name}_{self.uid}", list(shape), dt))

    def build(self):
        nc = self.nc
        L = DEPTH
        d = {}
        d["xa"] = self.din("xa", [128, KC, TA])
        d["xo"] = self.din("xo", [128, KC, TB])
        d["cT"] = self.din("cT", [128, KC, 2])
        d["w_ada"] = self.din("w_ada", [L, 12, 128, KC, 512])
        d["b_ada"] = self.din("b_ada", [L, 128, 48])
        d["nw"] = self.din("nw", [L, 128, 4, KC])
        d["w_in"] = self.din("w_in", [L, NCH, 128, KC, 128])
        d["w_ukn"] = self.din("w_ukn", [L, 128, 4, 2, 128])
        d["w_ukv"] = self.din("w_ukv", [L, 128, 2, 512])
        d["w_uq"] = self.din("w_uq", [L, 128, 8, 2, 96])
        d["w_uqs"] = self.din("w_uqs", [L, 128, 8, 2, 96])
        d["qn"] = self.din("qn", [L, 128, 2])
        d["kvn"] = self.din("kvn", [L, 128, 2])
        d["dlam"] = self.din("dlam", [L, 1, 256])
        d["dnorm"] = self.din("dnorm", [L, 64, 2])
        d["sink"] = self.din("sink", [L, 1, 8])
        d["w_br"] = self.din("w_br", [L, 3, 128, 4, 1024])
        d["w_o"] = self.din("w_o", [L, 128, KC, 1024])
        d["w_gu"] = self.din("w_gu", [L, NFC, 128, KC, 256])
        d["w_dn"] = self.din("w_dn", [L, KC, 128, NFC, 128])
        d["ident"] = self.din("ident", [128, 128])
        d["rope64"] = self.din("rope64", [2, 128, NK])
        d["rope32"] = self.din("rope32", [2, 128, NK])
        d["maskb"] = self.din("maskb", [128, 4, 512])
        d["sel"] = self.din("sel", [128, 2])
        self.d = d
        self.gin = [nc.dram_tensor(f"gin{i}", [128 * 2, SOWN], F32, kind="Internal").ap() for i in range(4)]
        self.gout = [nc.dram_tensor(f"gout{i}", [2 * 128 * 2, SOWN], F32, kind="Internal").ap() for i in range(4)]
        self.out = nc.dram_tensor("xout", [128, KC, TA], F32, kind="ExternalOutput").ap()
        s = {}
        s["KN"] = self.dscr("KN", [512, NK])
        s["KR"] = self.dscr("KR", [32, NK])
        s["VM"] = self.dscr("VM", [128, NKT, 520])
        s["DK"] = self.dscr("DK", [512, NK])
        s["DV"] = self.dscr("DV", [128, NKT, 520])
        s["SK"] = self.dscr("SK", [128, NK])
        s["SV"] = self.dscr("SV", [128, NKT, 130])
        s["QM"] = self.dscr("QM", [8, 96, TA])
        s["DQ"] = self.dscr("DQ", [512, TA])
        s["SQ"] = self.dscr("SQ", [512, TA])
        s["GT"] = self.dscr("GT", [3072, TA])
        s["OA"] = self.dscr("OA", [512, TA])
        s["OB"] = self.dscr("OB", [512, TA])
        s["OC"] = self.dscr("OC", [512, TA])
        self.s = s

        with ExitStack() as st:
            self.S = S = Sched(nc, st)
            self.ps = [st.enter_context(nc.psum_tensor(f"ps{i}", [128, 512], F32)) for i in range(8)]
            self.X = self.sb(st, "X", [128, KC, TA], F32)
            self.onesD = self.sb(st, "onesD", [128, 128], BF16)
            self.ones256 = self.sb(st, "ones256", [128, 128], BF16)
            self.ones128 = self.sb(st, "ones128", [128, 128], BF16)
            self.onesb = self.sb(st, "onesb", [128, 128], BF16)
            self.onesf = self.sb(st, "onesf", [128, 128], F32)
            self.zrow = self.sb(st, "zrow", [128, 128], F32)
            self.ident = self.sb(st, "ident", [128, 128], BF16)
            self.DER = self.sb(st, "DER", [128, 2, 6, KC], F32)
            self.MOD = self.sb(st, "MOD", [128, 2, 48], F32)
            self.small = self.sb(st, "small", [128, 256], F32)
            for t, v in ((self.onesD, 1.0 / 1024), (self.ones256, 1.0 / 256), (self.ones128, 1.0 / 128),
                         (self.onesb, 1.0), (self.onesf, 1.0), (self.zrow, 0.0)):
                S.op("dve", lambda t=t, v=v: nc.vector.memset(t[:], v), writes=["consts"])
            S.dma("pool", self.ident[:], d["ident"], writes=["consts"])
            S.dma("sp", self.X[:], d["xa"], writes=["X"])
            self.sel = self.sb(st, "sel", [128, 2], F32)
            S.dma("sp", self.sel[:], d["sel"], writes=["consts"])
            for li, l in enumerate(self.layers):
                last = (l == DEPTH - 1)
                self.gathered = (li > 0)
                if li > 0:
                    self.exchange()
                self.layer(l, last)
            S.barrier()
            if self.dbg:
                dd = nc.dram_tensor("dbg_der", [128, 2, 6, KC], F32, kind="ExternalOutput").ap()
                S.dma("sp", dd, self.DER[:], reads=["DER"], writes=["dbgder"])
            S.dma("sp", self.out, self.X[:], reads=["X"], writes=["out"])
            S.barrier(engines=("sp",))
            self.stats = (S.n_ins, S.n_wait, S.nsem)
        return nc

    def exchange(self):
        nc, S = self.nc, self.S
        S.barrier()
        for i in range(4):
            S.dma("pool", self.gin[i].rearrange("(p k) t -> p k t", k=2), self.X[:, 2 * i:2 * i + 2, NCTX:TA],
                  reads=["X"], writes=[("gin", i)])
        for i in range(4):
            S.collective(lambda: nc.gpsimd.collective_compute(
                "AllGather", ALU.bypass, replica_groups=[[0, 1], [2, 3], [4, 5], [6, 7]],
                ins=[self.gin[i]], outs=[self.gout[i]]), reads=[("gin", i)], writes=[("gout", i)])

    def layer(self, l, last):
        self.stop = self.stop_cfg if (self.stop_layer is None or self.stop_layer == l) else None
        self.l = l
        self.last = last
        S = self.S
        self.phase_ada(l)
        S.barrier()
        if self.stop == "ada":
            return
        self.phase_proj(l)
        S.barrier()
        if self.stop in ("proj", "projA"):
            return
        self.phase_attn(l)
        S.barrier()
        if self.stop in ("attn", "mla", "diff"):
            return
        self.phase_merge(l)
        S.barrier()
        if self.stop == "merge":
            return
        self.phase_ffn(l)
        S.barrier()

    def qblocks(self):
        blks = []
        if not self.last:
            blks.append((0, NCTX, 1))
        for i in range(4):
            blks.append((NCTX + 512 * i, 512, 0))
        return blks

    def phase_ada(self, l):
        nc, S, d = self.nc, self.S, self.d
        with ExitStack() as ph:
            wb = [self.sb(ph, "wada", [128, KC, 512], BF16) for _ in range(2)]
            cT = self.sb(ph, "cT", [128, KC, 2], F32)
            silu = self.sb(ph, "silu", [128, KC, 2], BF16)
            bada = self.sb(ph, "bada", [128, 48], F32)
            nw = self.sb(ph, "nw", [128, 4, KC], F32)
            S.dma("sp", cT[:], d["cT"], writes=["cT"])
            S.dma("sp", bada[:], d["b_ada"][l], writes=["bada"])
            S.dma("sp", nw[:], d["nw"][l], writes=["nw"])
            S.op("act", lambda: nc.scalar.activation(out=silu[:], in_=cT[:], func=AF.Silu),
                 reads=["cT"], writes=["silu"])
            psm = self.ps[0]
            for blk in range(12):
                w = wb[blk % 2]
                wk = ("wada", blk % 2)
                S.dma("pool", w[:], d["w_ada"][l, blk], writes=[wk])
                for j in range(4):
                    oc = blk * 4 + j
                    for kc in range(KC):
                        S.op("pe", lambda: nc.tensor.matmul(psm[:, oc * 2:oc * 2 + 2], w[:, kc, j * 128:(j + 1) * 128],
                                                            silu[:, kc, :], start=(kc == 0), stop=(kc == KC - 1)),
                             reads=[wk, "silu"], writes=["ps0"])
            pv = psm[:, 0:96].rearrange("p (a s) -> p a s", s=2)
            for s_ in range(2):
                S.op("dve", lambda: nc.vector.tensor_tensor(out=self.MOD[:, s_, :], in0=pv[:, :, s_], in1=bada[:],
                                                            op=ALU.add),
                     reads=["ps0", "bada"], writes=["MOD"])
            M, DER = self.MOD, self.DER
            for s_ in range(2):
                for (j, sc_i, sh_i, g_i, npre, npost) in ((0, 1, 0, 2, 0, 1), (3, 4, 3, 5, 2, 3)):
                    S.op("dve", lambda: nc.vector.scalar_tensor_tensor(
                        out=DER[:, s_, j, :], in0=M[:, s_, sc_i * 8:(sc_i + 1) * 8], scalar=1.0, in1=nw[:, npre, :],
                        op0=ALU.add, op1=ALU.mult), reads=["MOD", "nw"], writes=["DER"])
                    S.op("dve", lambda: nc.vector.tensor_copy(out=DER[:, s_, j + 1, :], in_=M[:, s_, sh_i * 8:(sh_i + 1) * 8]),
                         reads=["MOD"], writes=["DER"])
                    S.op("dve", lambda: nc.vector.tensor_tensor(out=DER[:, s_, j + 2, :], in0=M[:, s_, g_i * 8:(g_i + 1) * 8],
                                                                in1=nw[:, npost, :], op=ALU.mult),
                         reads=["MOD", "nw"], writes=["DER"])
            S.barrier()

    def emit_norm_h(self, tl, src, src_keys, T, dst, dst_keys, s_, j):
        nc, S = self.nc, self.S
        pst = self.ps[7]
        for kc in range(KC):
            sq, sqk = tl["sq"][tl["sqi"] % len(tl["sq"])], ("sq", tl["sqi"] % len(tl["sq"]))
            tl["sqi"] += 1
            S.op("act", lambda: nc.scalar.activation(out=sq[:, :T], in_=src(kc), func=AF.Square),
                 reads=src_keys, writes=[sqk])
            S.op("pe", lambda: nc.tensor.matmul(pst[:, :T], self.onesD[:], sq[:, :T], start=(kc == 0), stop=(kc == KC - 1)),
                 reads=[sqk, "consts"], writes=["ps7"])
        rstd = tl["rstd"]
        S.op("act", lambda: nc.scalar.activation(out=rstd[:, :T], in_=pst[:, :T], func=AF.Sqrt, bias=EPS, scale=1.0),
             reads=["ps7"], writes=["rstd"])
        S.op("dve", lambda: nc.vector.reciprocal(out=rstd[:, :T], in_=rstd[:, :T]), reads=["rstd"], writes=["rstd"])
        for kc in range(KC):
            t, tk = tl["t"][tl["ti"] % len(tl["t"])], ("t", tl["ti"] % len(tl["t"]))
            tl["ti"] += 1
            S.op("dve", lambda: nc.vector.scalar_tensor_tensor(
                out=t[:, :T], in0=src(kc), scalar=self.DER[:, s_, j, kc:kc + 1], in1=rstd[:, :T],
                op0=ALU.mult, op1=ALU.mult), reads=list(src_keys) + ["rstd", "DER"], writes=[tk])
            S.op("act", lambda: nc.scalar.activation(out=dst(kc), in_=t[:, :T], func=AF.Identity,
                                                     bias=self.DER[:, s_, j + 1, kc:kc + 1], scale=1.0),
                 reads=[tk, "DER"], writes=dst_keys)

    def phase_proj(self, l):
        nc, S, d, s = self.nc, self.S, self.d, self.s
        with ExitStack() as ph:
            hT = self.sb(ph, "hT", [128, KC, TA], BF16)
            xob = self.sb(ph, "xob", [128, KC, 512], F32)
            tl = {"sq": [self.sb(ph, "sq", [128, 512], BF16) for _ in range(2)], "sqi": 0,
                  "t": [self.sb(ph, "t", [128, 512], F32) for _ in range(4)], "ti": 0,
                  "rstd": self.sb(ph, "rstd", [128, 512], F32)}
            wbuf = [self.sb(ph, "wbuf", [128, KC, 128], BF16) for _ in range(4)]
            wdv = self.sb(ph, "wdv", [128, 4, KC, 128], BF16)
            wukn = self.sb(ph, "wukn", [128, 4, 2, 128], BF16)
            wukv = self.sb(ph, "wukv", [128, 2, 512], BF16)
            wuq = self.sb(ph, "wuq", [128, 8, 2, 96], BF16)
            wuqs = self.sb(ph, "wuqs", [128, 8, 2, 96], BF16)
            qn = self.sb(ph, "qn", [128, 2], F32)
            kvn = self.sb(ph, "kvn", [128, 2], F32)
            cf = self.sb(ph, "cf", [128, 2, 512], F32)
            cn = self.sb(ph, "cn", [128, 2, 512], BF16)
            tab = [self.sb(ph, "tab", [128, 2, 512], F32) for _ in range(3)]
            stg = [self.sb(ph, "stg", [128, 512], BF16) for _ in range(4)]
            vst = [self.sb(ph, "vst", [128, 8, 65], BF16) for _ in range(2)]
            svst = [self.sb(ph, "svst", [128, 2, 65], BF16) for _ in range(2)]
            st_ = {"tab": 0, "stg": 0, "vst": 0, "svst": 0, "ps": 0}

            for v in vst + svst:
                S.op("dve", lambda: nc.vector.memset(v[:], 1.0), writes=["vst_init"])
            S.dma("pool", wukn[:], d["w_ukn"][l], writes=["wsmall"])
            S.dma("pool", wukv[:], d["w_ukv"][l], writes=["wsmall"])
            S.dma("pool", wuq[:], d["w_uq"][l], writes=["wsmall"])
            S.dma("pool", wuqs[:], d["w_uqs"][l], writes=["wsmall"])
            S.dma("sp", qn[:], d["qn"][l], writes=["wsmall"])
            S.dma("sp", kvn[:], d["kvn"][l], writes=["wsmall"])

            def nextps():
                i = st_["ps"] % 6
                st_["ps"] += 1
                return self.ps[i], ("ps", i)

            def nextstg():
                i = st_["stg"] % len(stg)
                st_["stg"] += 1
                return stg[i], ("stg", i)

            class WStream:
                def __init__(ws, seq):
                    ws.seq = list(seq)
                    ws.issued = 0
                    ws.used = 0
                    ws.slots = {}

                def _issue(ws):
                    i = ws.issued
                    bi = i % len(wbuf)
                    S.dma("pool", wbuf[bi][:], d["w_in"][l, ws.seq[i]], writes=[("wbuf", bi)])
                    ws.slots[i] = bi
                    ws.issued += 1

                def get(ws, ci, n_live=1):
                    i = ws.used
                    assert ws.seq[i] == ci, (ws.seq[i], ci)
                    ws.used += 1
                    while ws.issued < len(ws.seq) and ws.issued <= i + 2:
                        ws._issue()
                    bi = ws.slots[i]
                    return wbuf[bi], ("wbuf", bi)

            def load_tab(name, kcol, T):
                i = st_["tab"] % len(tab)
                st_["tab"] += 1
                S.dma("act", tab[i][:, :, :T], d[name][:, :, kcol:kcol + T].rearrange("a p t -> p a t"),
                      writes=[("tab", i)])
                return tab[i], ("tab", i)

            def fm_group(psb, psk, wt, wk, M, blk):
                c0, T, bi = blk[0], blk[1], blk[3]
                for kc in range(KC):
                    S.op("pe", lambda: nc.tensor.matmul(psb[0:M, :T], wt[:, kc, 0:M], hT[:, kc, c0:c0 + T],
                                                        start=(kc == 0), stop=(kc == KC - 1)),
                         reads=[wk, ("hT", bi)], writes=[psk])

            def rope_epi(psz, pzk, pss, psk, p0, p1, T, tb, tbk, out_stg, ostk):
                t1, t1k = tl["t"][tl["ti"] % 4], ("t", tl["ti"] % 4)
                tl["ti"] += 1
                t2, t2k = tl["t"][tl["ti"] % 4], ("t", tl["ti"] % 4)
                tl["ti"] += 1
                S.op("dve", lambda: nc.vector.tensor_tensor(out=t1[p0:p1, :T], in0=psz[p0:p1, :T], in1=tb[p0:p1, 0, :T], op=ALU.mult),
                     reads=[pzk, tbk], writes=[t1k])
                S.op("dve", lambda: nc.vector.tensor_tensor(out=t2[p0:p1, :T], in0=pss[p0:p1, :T], in1=tb[p0:p1, 1, :T], op=ALU.mult),
                     reads=[psk, tbk], writes=[t2k])
                S.op("pool", lambda: nc.gpsimd.tensor_tensor(out=out_stg[p0:p1, :T], in0=t1[p0:p1, :T], in1=t2[p0:p1, :T], op=ALU.add),
                     reads=[t1k, t2k], writes=[ostk])

            def lowrank(chs, blk, normw):
                c0, T = blk[0], blk[1]
                pst = self.ps[7]
                banks = []
                for oc in range(2):
                    psb, psk = nextps()
                    fm_group(psb, psk, chs[oc][0], chs[oc][1], 128, blk)
                    banks.append((psb, psk))
                for oc in range(2):
                    psb, psk = banks[oc]
                    S.op("act", lambda: nc.scalar.copy(out=cf[:, oc, :T], in_=psb[:, :T]), reads=[psk], writes=[("cf", oc)])
                    sq, sqk = tl["sq"][tl["sqi"] % 2], ("sq", tl["sqi"] % 2)
                    tl["sqi"] += 1
                    S.op("act", lambda: nc.scalar.activation(out=sq[:, :T], in_=psb[:, :T], func=AF.Square),
                         reads=[psk], writes=[sqk])
                    S.op("pe", lambda: nc.tensor.matmul(pst[:, :T], self.ones256[:], sq[:, :T], start=(oc == 0), stop=(oc == 1)),
                         reads=[sqk, "consts"], writes=["ps7"])
                rstd = tl["rstd"]
                S.op("act", lambda: nc.scalar.activation(out=rstd[:, :T], in_=pst[:, :T], func=AF.Sqrt, bias=EPS, scale=1.0),
                     reads=["ps7"], writes=["rstd"])
                S.op("dve", lambda: nc.vector.reciprocal(out=rstd[:, :T], in_=rstd[:, :T]), reads=["rstd"], writes=["rstd"])
                for oc in range(2):
                    S.op("dve", lambda: nc.vector.scalar_tensor_tensor(
                        out=cn[:, oc, :T], in0=cf[:, oc, :T], scalar=normw[:, oc:oc + 1], in1=rstd[:, :T],
                        op0=ALU.mult, op1=ALU.mult), reads=[("cf", oc), "rstd", "wsmall"], writes=["cn"])

            KV_SEQ = [CH_CKV[0], CH_CKV[1], CH_KR, CH_KRS]
            for i in range(4):
                KV_SEQ += [CH_DK[i], CH_DKS[i]]
            KV_SEQ += [CH_SK, CH_SKS, CH_SV]
            Q_SEQ = [CH_CQ[0], CH_CQ[1]]
            for i in range(4):
                Q_SEQ += [CH_DQ[i], CH_DQS[i]]
            for i in range(4):
                Q_SEQ += [CH_SQ[i], CH_SQS[i]]
            Q_SEQ += [CH_G0 + i for i in range(24)]

            def kv_pass(ws, blocks, kbase):
                for i in range(4):
                    S.dma("pool", wdv[:, i, :, :], d["w_in"][l, CH_DV[i]], writes=["wdv"])
                chs = [ws.get(CH_CKV[0]), ws.get(CH_CKV[1])]
                for blk in blocks:
                    c0, T = blk[0], blk[1]
                    kcol = kbase + c0
                    lowrank(chs, blk, kvn)
                    for j in range(4):
                        psb, psk = nextps()
                        for k2 in range(2):
                            S.op("pe", lambda: nc.tensor.matmul(psb[:, :T], wukn[:, j, k2, :], cn[:, k2, :T],
                                                                start=(k2 == 0), stop=(k2 == 1)),
                                 reads=["wsmall", "cn"], writes=[psk])
                        sg, sgk = nextstg()
                        S.op("act", lambda: nc.scalar.copy(out=sg[:, :T], in_=psb[:, :T]), reads=[psk], writes=[sgk])
                        S.dma("sp", s["KN"][j * 128:(j + 1) * 128, kcol:kcol + T], sg[:, :T], reads=[sgk])
                    for tt in range(T // 128):
                        psb, psk = nextps()
                        for k2 in range(2):
                            S.op("pe", lambda: nc.tensor.matmul(psb[:, :], cn[:, k2, tt * 128:(tt + 1) * 128], wukv[:, k2, :],
                                                                start=(k2 == 0), stop=(k2 == 1)),
                                 reads=["wsmall", "cn"], writes=[psk])
                        vi = st_["vst"] % 2
                        st_["vst"] += 1
                        S.op("dve", lambda: nc.vector.tensor_copy(out=vst[vi][:, :, 0:64],
                                                                  in_=psb[:, :].rearrange("p (h e) -> p h e", e=64)),
                             reads=[psk, "vst_init"], writes=[("vst", vi)])
                        S.dma("sp", s["VM"][:, (kcol // 128) + tt, :], vst[vi][:].rearrange("p h e -> p (h e)"),
                              reads=[("vst", vi)])
                wz, wzk = ws.get(CH_KR)
                wsw, wswk = ws.get(CH_KRS)
                for blk in blocks:
                    c0, T = blk[0], blk[1]
                    kcol = kbase + c0
                    pa, pak = nextps()
                    fm_group(pa, pak, wz, wzk, 32, blk)
                    pb, pbk = nextps()
                    fm_group(pb, pbk, wsw, wswk, 32, blk)
                    tb, tbk = load_tab("rope32", kcol, T)
                    sg, sgk = nextstg()
                    rope_epi(pa, pak, pb, pbk, 0, 32, T, tb, tbk, sg, sgk)
                    S.dma("sp", s["KR"][:, kcol:kcol + T], sg[0:32, :T], reads=[sgk])
                for (cz, cs, dst, r0) in [(CH_DK[i], CH_DKS[i], "DK", i * 128) for i in range(4)] + [(CH_SK, CH_SKS, "SK", 0)]:
                    wz, wzk = ws.get(cz)
                    wsw, wswk = ws.get(cs)
                    for blk in blocks:
                        c0, T = blk[0], blk[1]
                        kcol = kbase + c0
                        pa, pak = nextps()
                        fm_group(pa, pak, wz, wzk, 128, blk)
                        pb, pbk = nextps()
                        fm_group(pb, pbk, wsw, wswk, 128, blk)
                        tb, tbk = load_tab("rope64", kcol, T)
                        sg, sgk = nextstg()
                        rope_epi(pa, pak, pb, pbk, 0, 128, T, tb, tbk, sg, sgk)
                        S.dma("sp", s[dst][r0:r0 + 128, kcol:kcol + T], sg[:, :T], reads=[sgk])
                wz, wzk = ws.get(CH_SV)
                for blk in blocks:
                    c0, T, bi = blk[0], blk[1], blk[3]
                    kcol = kbase + c0
                    for tt in range(T // 128):
                        psb, psk = nextps()
                        for kc in range(KC):
                            S.op("pe", lambda: nc.tensor.matmul(psb[:, 0:128], hT[:, kc, c0 + tt * 128:c0 + (tt + 1) * 128],
                                                                wz[:, kc, :], start=(kc == 0), stop=(kc == KC - 1)),
                                 reads=[wzk, ("hT", bi)], writes=[psk])
                        vi = st_["svst"] % 2
                        st_["svst"] += 1
                        S.op("dve", lambda: nc.vector.tensor_copy(out=svst[vi][:, :, 0:64],
                                                                  in_=psb[:, 0:128].rearrange("p (h e) -> p h e", e=64)),
                             reads=[psk, "vst_init"], writes=[("svst", vi)])
                        S.dma("sp", s["SV"][:, (kcol // 128) + tt, :], svst[vi][:].rearrange("p h e -> p (h e)"),
                              reads=[("svst", vi)])
                for blk in blocks:
                    c0, T, bi = blk[0], blk[1], blk[3]
                    kcol = kbase + c0
                    for tt in range(T // 128):
                        psb, psk = nextps()
                        for kc in range(KC):
                            S.op("pe", lambda: nc.tensor.matmul(psb[:, :], hT[:, kc, c0 + tt * 128:c0 + (tt + 1) * 128],
                                                                wdv[:, :, kc, :], start=(kc == 0), stop=(kc == KC - 1)),
                                 reads=["wdv", ("hT", bi)], writes=[psk])
                        vi = st_["vst"] % 2
                        st_["vst"] += 1
                        S.op("act", lambda: nc.scalar.copy(out=vst[vi][:, :, 0:64],
                                                           in_=psb[:, :].rearrange("p (h e) -> p h e", e=64)),
                             reads=[psk, "vst_init"], writes=[("vst", vi)])
                        S.dma("sp", s["DV"][:, (kcol // 128) + tt, :], vst[vi][:].rearrange("p h e -> p (h e)"),
                              reads=[("vst", vi)])

            def q_pass(ws, blocks):
                chs = [ws.get(CH_CQ[0]), ws.get(CH_CQ[1])]
                for blk in blocks:
                    c0, T = blk[0], blk[1]
                    lowrank(chs, blk, qn)
                    tb, tbk = load_tab("rope32", c0, T)
                    for h in range(8):
                        pa, pak = nextps()
                        pb, pbk = nextps()
                        for k2 in range(2):
                            S.op("pe", lambda: nc.tensor.matmul(pa[0:96, :T], wuq[:, h, k2, :], cn[:, k2, :T],
                                                                start=(k2 == 0), stop=(k2 == 1)),
                                 reads=["wsmall", "cn"], writes=[pak])
                        for k2 in range(2):
                            S.op("pe", lambda: nc.tensor.matmul(pb[0:96, :T], wuqs[:, h, k2, :], cn[:, k2, :T],
                                                                start=(k2 == 0), stop=(k2 == 1)),
                                 reads=["wsmall", "cn"], writes=[pbk])
                        sg, sgk = nextstg()
                        S.op("act", lambda: nc.scalar.copy(out=sg[0:64, :T], in_=pa[0:64, :T]), reads=[pak], writes=[sgk])
                        rope_epi(pa, pak, pb, pbk, 64, 96, T, tb, tbk, sg, sgk)
                        S.dma("sp", s["QM"][h, :, c0:c0 + T], sg[0:96, :T], reads=[sgk])
                for (cz, cs, dst, r0) in [(CH_DQ[i], CH_DQS[i], "DQ", i * 128) for i in range(4)] + \
                                         [(CH_SQ[i], CH_SQS[i], "SQ", i * 128) for i in range(4)]:
                    wz, wzk = ws.get(cz)
                    wsw, wswk = ws.get(cs)
                    for blk in blocks:
                        c0, T = blk[0], blk[1]
                        pa, pak = nextps()
                        fm_group(pa, pak, wz, wzk, 128, blk)
                        pb, pbk = nextps()
                        fm_group(pb, pbk, wsw, wswk, 128, blk)
                        tb, tbk = load_tab("rope64", c0, T)
                        sg, sgk = nextstg()
                        rope_epi(pa, pak, pb, pbk, 0, 128, T, tb, tbk, sg, sgk)
                        S.dma("sp", s[dst][r0:r0 + 128, c0:c0 + T], sg[:, :T], reads=[sgk])
                for gi in range(24):
                    wz, wzk = ws.get(CH_G0 + gi)
                    for blk in blocks:
                        c0, T = blk[0], blk[1]
                        pa, pak = nextps()
                        fm_group(pa, pak, wz, wzk, 128, blk)
                        sg, sgk = nextstg()
                        S.op("act", lambda: nc.scalar.activation(out=sg[:, :T], in_=pa[:, :T], func=AF.Sigmoid),
                             reads=[pak], writes=[sgk])
                        S.dma("sp", s["GT"][gi * 128:(gi + 1) * 128, c0:c0 + T], sg[:, :T], reads=[sgk])

            blocksA = [(0, NCTX, 1, 0)] + [(NCTX + 512 * i, 512, 0, i + 1) for i in range(4)]
            for (c0, T, s_, bi) in blocksA:
                self.emit_norm_h(tl, lambda kc: self.X[:, kc, c0:c0 + T], ["X"], T,
                                 lambda kc: hT[:, kc, c0:c0 + T], [("hT", bi)], s_, 0)
            ws = WStream(KV_SEQ + Q_SEQ)
            kv_pass(ws, blocksA, 0)
            q_pass(ws, [b for b in blocksA if not (self.last and b[2] == 1)])
            S.barrier()
            blocksB = [(512 * i, 512, 0, i) for i in range(4)]
            gvs = [g.rearrange("(r p k) t -> r p k t", r=2, k=2) for g in self.gout]
            for (c0, T, s_, bi) in blocksB:
                if not self.gathered:
                    S.dma("sp", xob[:], d["xo"][:, :, c0:c0 + T], writes=["xob"])
                else:
                    for kc in range(KC):
                        ta, tak = tl["t"][tl["ti"] % 4], ("t", tl["ti"] % 4)
                        tl["ti"] += 1
                        tb2, tbk2 = tl["t"][tl["ti"] % 4], ("t", tl["ti"] % 4)
                        tl["ti"] += 1
                        gv = gvs[kc // 2]
                        S.dma("sp", ta[:, :T], gv[0, :, kc % 2, c0:c0 + T], reads=[("gout", kc // 2)], writes=[tak])
                        S.dma("sp", tb2[:, :T], gv[1, :, kc % 2, c0:c0 + T], reads=[("gout", kc // 2)], writes=[tbk2])
                        S.op("dve", lambda: nc.vector.tensor_scalar(out=ta[:, :T], in0=ta[:, :T], scalar1=self.sel[:, 0:1],
                                                                    scalar2=None, op0=ALU.mult),
                             reads=[tak, "consts"], writes=[tak])
                        S.op("dve", lambda: nc.vector.scalar_tensor_tensor(out=xob[:, kc, :T], in0=tb2[:, :T],
                                                                           scalar=self.sel[:, 1:2], in1=ta[:, :T],
                                                                           op0=ALU.mult, op1=ALU.add),
                             reads=[tbk2, tak, "consts"], writes=["xob"])
                self.emit_norm_h(tl, lambda kc: xob[:, kc, :T], ["xob"], T,
                                 lambda kc: hT[:, kc, c0:c0 + T], [("hT", bi)], 0, 0)
            ws = WStream(KV_SEQ)
            kv_pass(ws, blocksB, TA)

    def phase_attn(self, l):
        nc, S, d, s = self.nc, self.S, self.d, self.s
        lam_init = 0.8 - 0.6 * math.exp(-0.3 * l)
        qbl = self.qblocks()
        with ExitStack() as ph:
            KT = [self.sb(ph, "KT", [128, NK], BF16) for _ in range(2)]
            QT = [self.sb(ph, "QT", [128, TA], BF16) for _ in range(2)]
            QZ = [[self.sb(ph, "QZ", [128, TA], BF16) for _ in range(2)] for _ in range(2)]
            PT = [self.sb(ph, "PT", [128, 512], BF16) for _ in range(6)]
            rec = self.sb(ph, "rec", [128, 512], F32)
            bcsb = [self.sb(ph, "bcsb", [128, 512], F32) for _ in range(2)]
            ub = [self.sb(ph, "ub", [128, 512], F32) for _ in range(6)]
            sqb = [self.sb(ph, "sqb", [128, 512], BF16) for _ in range(2)]
            ostg = [self.sb(ph, "ostg", [128, 512], BF16) for _ in range(4)]
            sm = self.small
            dl = self.sb(ph, "dl", [128, 256], F32)
            dnw = self.sb(ph, "dnw", [64, 2], F32)
            sk8 = self.sb(ph, "sk8", [128, 8], F32)
            sinkrow = self.sb(ph, "sinkrow", [128, 2, 512], F32)
            maskb = self.sb(ph, "maskb", [128, 4, 512], BF16)
            cnt = {"pt": 0, "ostg": 0, "sb": 0}
            phv = ExitStack()
            VA = self.sb(phv, "VA", [128, NKT, 520], BF16)

            S.dma("sp", dl[64:65, :], d["dlam"][l], writes=["dl"])
            S.dma("sp", dnw[:], d["dnorm"][l], writes=["dnw"])
            S.dma("sp", sk8[64:65, :], d["sink"][l], writes=["sk8"])
            S.dma("pool", maskb[:], d["maskb"], writes=["maskb"])
            for b_ in range(2):
                for e_ in range(2):
                    S.op("pool", lambda: nc.gpsimd.memset(QZ[b_][e_][:], 0.0), writes=[("QZ", b_, e_)])
            R = slice(64, 65)
            S.op("dve", lambda: nc.vector.tensor_tensor(out=sm[R, 0:64], in0=dl[R, 0:64], in1=dl[R, 64:128], op=ALU.mult),
                 reads=["dl"], writes=["sm_a"])
            S.op("dve", lambda: nc.vector.tensor_tensor(out=sm[R, 64:128], in0=dl[R, 128:192], in1=dl[R, 192:256], op=ALU.mult),
                 reads=["dl"], writes=["sm_b"])
            S.op("dve", lambda: nc.vector.reduce_sum(out=sm[R, 128:129], in_=sm[R, 0:64], axis=AX.X),
                 reads=["sm_a"], writes=["sm_c"])
            S.op("dve", lambda: nc.vector.reduce_sum(out=sm[R, 129:130], in_=sm[R, 64:128], axis=AX.X),
                 reads=["sm_b"], writes=["sm_d"])
            S.op("act", lambda: nc.scalar.activation(out=sm[R, 130:132], in_=sm[R, 128:130], func=AF.Exp),
                 reads=["sm_c", "sm_d"], writes=["sm_e"])
            S.op("dve", lambda: nc.vector.scalar_tensor_tensor(out=sm[R, 132:133], in0=sm[R, 131:132], scalar=-lam_init,
                                                               in1=sm[R, 130:131], op0=ALU.add, op1=ALU.subtract),
                 reads=["sm_e"], writes=["neglam"])
            S.op("dve", lambda: nc.vector.tensor_scalar(out=dnw[:], in0=dnw[:], scalar1=(1.0 - lam_init), scalar2=None,
                                                        op0=ALU.mult), reads=["dnw"], writes=["dnw"])
            S.op("act", lambda: nc.scalar.activation(out=sk8[R, :], in_=sk8[R, :], func=AF.Exp),
                 reads=["sk8"], writes=["sk8"])
            for g in range(2):
                for i in range(4):
                    S.op("dve", lambda: nc.vector.tensor_scalar(out=sinkrow[R, g, i * 128:(i + 1) * 128],
                                                                in0=self.zrow[R, 0:128],
                                                                scalar1=sk8[R, g * 4 + i:g * 4 + i + 1], scalar2=None,
                                                                op0=ALU.add),
                         reads=["sk8", "consts"], writes=["sinkrow"])

            pending = [None]

            def attn_block(n, T, qk_fn, scale, pv_fn, fin_fn):
                sb_list = sbank_cfg["banks"]
                LA = sbank_cfg["la"]
                sbanks = []

                def qk(i):
                    bi = sb_list[cnt["sb"] % len(sb_list)]
                    cnt["sb"] += 1
                    qk_fn(i, self.ps[bi], ("ps", bi))
                    sbanks.append(bi)
                for i in range(min(LA, n)):
                    qk(i)
                for i in range(n):
                    if i + LA < n:
                        qk(i + LA)
                    bi = sbanks[i]
                    pi = cnt["pt"] % len(PT)
                    cnt["pt"] += 1
                    S.op("act", lambda: nc.scalar.activation(out=PT[pi][:, :T], in_=self.ps[bi][:, :T], func=AF.Exp, scale=scale),
                         reads=[("ps", bi)], writes=[("pt", pi)])
                    pv_fn(i, PT[pi], ("pt", pi))
                    if pending[0] is not None and i == min(3, n - 1):
                        pending[0]()
                        pending[0] = None
                if pending[0] is not None:
                    pending[0]()
                pending[0] = fin_fn

            def flush():
                if pending[0] is not None:
                    pending[0]()
                    pending[0] = None

            def next_ostg():
                i = cnt["ostg"] % len(ostg)
                cnt["ostg"] += 1
                return ostg[i], ("ostg", i)

            sbank_cfg = {"banks": [0, 1, 2, 5, 7], "la": 2}

            def next_sbank():
                return 3

            def bcast_recip(T, zsrc_fn, zkeys, scale_neglam=False):
                zsrc_fn()
                S.op("dve", lambda: nc.vector.reciprocal(out=rec[R, :T], in_=rec[R, :T]), reads=["rec"], writes=["rec"])
                if scale_neglam:
                    S.op("dve", lambda: nc.vector.tensor_scalar(out=rec[R, :T], in0=rec[R, :T], scalar1=sm[R, 132:133],
                                                                scalar2=None, op0=ALU.mult),
                         reads=["rec", "neglam"], writes=["rec"])
                bb = next_sbank()
                S.op("pe", lambda: nc.tensor.matmul(self.ps[bb][:, :T], self.onesf[R, 0:128], rec[R, :T], start=True, stop=True),
                     reads=["rec", "consts"], writes=[("ps", bb)])
                S.op("act", lambda: nc.scalar.copy(out=bcsb[0][:, :T], in_=self.ps[bb][:, :T]),
                     reads=[("ps", bb)], writes=[("bcsb", 0)])

            S.dma("sp", VA[:, 0:17, :], s["VM"][:, 0:17, :], writes=["VA"])
            S.dma("sp", VA[:, 17:34, :], s["VM"][:, 17:34, :], writes=["VA"])

            def load_mla(h):
                b = h % 2
                S.dma("sp", KT[b][0:64, :], s["KN"][h * 64:(h + 1) * 64, :], writes=[("KT", b)])
                S.dma("sp", KT[b][64:96, :], s["KR"][:, :], writes=[("KT", b)])
                S.dma("sp", QT[b][0:96, :], s["QM"][h], writes=[("QT", b)])
            load_mla(0)
            accsel = [0]
            for h in range(8):
                if h + 1 < 8:
                    load_mla(h + 1)
                b = h % 2
                for (c0, T, s_) in qbl:
                    ktiles = [0, 1] if s_ == 1 else list(range(NKT))
                    n = len(ktiles)
                    ai = 4 + 2 * (accsel[0] % 2)
                    accsel[0] += 1
                    acc, acck = self.ps[ai], ("ps", ai)

                    def qk_fn(i, psb, psk, ktiles=ktiles, c0=c0, T=T, b=b):
                        kt = ktiles[i]
                        S.op("pe", lambda: nc.tensor.matmul(psb[:, :T], KT[b][0:96, kt * 128:(kt + 1) * 128],
                                                            QT[b][0:96, c0:c0 + T], start=True, stop=True),
                             reads=[("KT", b), ("QT", b)], writes=[psk])

                    def pv_fn(i, pt, ptk, ktiles=ktiles, T=T, h=h, acc=acc, acck=acck, n=n):
                        kt = ktiles[i]
                        S.op("pe", lambda: nc.tensor.matmul(acc[0:65, :T], VA[:, kt, h * 65:(h + 1) * 65], pt[:, :T],
                                                            start=(i == 0), stop=(i == n - 1)),
                             reads=[ptk, "VA"], writes=[acck])

                    def fin(c0=c0, T=T, h=h, acc=acc, acck=acck):
                        bcast_recip(T, lambda: S.op("dve", lambda: nc.vector.tensor_copy(out=rec[R, :T], in_=acc[R, :T]),
                                                    reads=[acck], writes=["rec"]), [acck])
                        og, ogk = next_ostg()
                        S.op("dve", lambda: nc.vector.tensor_tensor(out=og[0:64, :T], in0=acc[0:64, :T], in1=bcsb[0][0:64, :T],
                                                                    op=ALU.mult),
                             reads=[acck, ("bcsb", 0)], writes=[ogk])
                        S.dma("pool", s["OA"][h * 64:(h + 1) * 64, c0:c0 + T], og[0:64, :T], reads=[ogk])
                    attn_block(n, T, qk_fn, MLA_SCALE, pv_fn, fin)
            flush()
            if self.stop == "mla":
                S.barrier()
                phv.close()
                return

            sbank_cfg["banks"] = [0, 1, 2]
            sbank_cfg["la"] = 2
            S.dma("sp", VA[:, 0:17, :], s["DV"][:, 0:17, :], writes=["VA"])
            S.dma("sp", VA[:, 17:34, :], s["DV"][:, 17:34, :], writes=["VA"])

            def load_diff(j):
                b = j % 2
                S.dma("sp", KT[b][:, :], s["DK"][j * 128:(j + 1) * 128, :], writes=[("KT", b)])
                for e in range(2):
                    S.dma("sp", QZ[b][e][e * 64:(e + 1) * 64, :], s["DQ"][j * 128 + e * 64:j * 128 + (e + 1) * 64, :],
                          writes=[("QZ", b, e)])
            load_diff(0)
            for j in range(4):
                if j + 1 < 4:
                    load_diff(j + 1)
                b = j % 2
                for (c0, T, s_) in qbl:
                    ktiles = [0, 1] if s_ == 1 else list(range(NKT))
                    n = len(ktiles)
                    for e in range(2):
                        ai = 4 + 2 * (accsel[0] % 2)
                        accsel[0] += 1
                        acca, accak = self.ps[ai], ("ps", ai)
                        accb, accbk = self.ps[ai + 1], ("ps", ai + 1)

                        def qk_fn(i, psb, psk, ktiles=ktiles, c0=c0, T=T, b=b, e=e):
                            kt = ktiles[i]
                            S.op("pe", lambda: nc.tensor.matmul(psb[:, :T], KT[b][:, kt * 128:(kt + 1) * 128],
                                                                QZ[b][e][:, c0:c0 + T], start=True, stop=True),
                                 reads=[("KT", b), ("QZ", b, e)], writes=[psk])

                        def pv_fn(i, pt, ptk, ktiles=ktiles, T=T, j=j, acca=acca, accak=accak, accb=accb, accbk=accbk, n=n):
                            kt = ktiles[i]
                            S.op("pe", lambda: nc.tensor.matmul(acca[0:65, :T], VA[:, kt, (2 * j) * 65:(2 * j + 1) * 65], pt[:, :T],
                                                                start=(i == 0), stop=(i == n - 1)),
                                 reads=[ptk, "VA"], writes=[accak])
                            S.op("pe", lambda: nc.tensor.matmul(accb[0:65, :T], VA[:, kt, (2 * j + 1) * 65:(2 * j + 2) * 65], pt[:, :T],
                                                                start=(i == 0), stop=(i == n - 1)),
                                 reads=[ptk, "VA"], writes=[accbk])

                        def fin(c0=c0, T=T, j=j, e=e, acca=acca, accak=accak, accb=accb, accbk=accbk):
                            bcast_recip(T, lambda: S.op("dve", lambda: nc.vector.tensor_copy(out=rec[R, :T], in_=acca[R, :T]),
                                                        reads=[accak], writes=["rec"]), [accak], scale_neglam=(e == 1))
                            S.op("dve", lambda: nc.vector.tensor_tensor(out=ub[2 * e][0:64, :T], in0=acca[0:64, :T],
                                                                        in1=bcsb[0][0:64, :T], op=ALU.mult),
                                 reads=[accak, ("bcsb", 0)], writes=[("ub", 2 * e)])
                            S.op("dve", lambda: nc.vector.tensor_tensor(out=ub[2 * e + 1][0:64, :T], in0=accb[0:64, :T],
                                                                        in1=bcsb[0][0:64, :T], op=ALU.mult),
                                 reads=[accbk, ("bcsb", 0)], writes=[("ub", 2 * e + 1)])
                            if e == 1:
                                for hh in range(2):
                                    S.op("pool", lambda: nc.gpsimd.tensor_tensor(out=ub[4 + hh][0:64, :T], in0=ub[hh][0:64, :T],
                                                                                 in1=ub[2 + hh][0:64, :T], op=ALU.add),
                                         reads=[("ub", hh), ("ub", 2 + hh)], writes=[("ub", 4 + hh)])
                                    S.op("act", lambda: nc.scalar.activation(out=sqb[hh][0:64, :T], in_=ub[4 + hh][0:64, :T],
                                                                             func=AF.Square),
                                         reads=[("ub", 4 + hh)], writes=[("sqb", hh)])
                                mb = next_sbank()
                                for hh in range(2):
                                    S.op("pe", lambda: nc.tensor.matmul(self.ps[mb][:, :T], self.ones128[0:64, :], sqb[hh][0:64, :T],
                                                                        start=(hh == 0), stop=(hh == 1)),
                                         reads=[("sqb", hh), "consts"], writes=[("ps", mb)])
                                S.op("act", lambda: nc.scalar.activation(out=bcsb[1][0:64, :T], in_=self.ps[mb][0:64, :T], func=AF.Ln, bias=EPS),
                                     reads=[("ps", mb)], writes=[("bcsb", 1)])
                                S.op("act", lambda: nc.scalar.activation(out=bcsb[1][0:64, :T], in_=bcsb[1][0:64, :T], func=AF.Exp, scale=-0.5),
                                     reads=[("bcsb", 1)], writes=[("bcsb", 1)])
                                for hh in range(2):
                                    og, ogk = next_ostg()
                                    S.op("dve", lambda: nc.vector.scalar_tensor_tensor(
                                        out=og[0:64, :T], in0=ub[4 + hh][0:64, :T], scalar=dnw[:, hh:hh + 1], in1=bcsb[1][0:64, :T],
                                        op0=ALU.mult, op1=ALU.mult), reads=[("ub", 4 + hh), ("bcsb", 1), "dnw"], writes=[ogk])
                                    S.dma("pool", s["OB"][j * 128 + hh * 64:j * 128 + (hh + 1) * 64, c0:c0 + T], og[0:64, :T],
                                          reads=[ogk])
                        attn_block(n, T, qk_fn, DH_SCALE, pv_fn, fin)
            flush()

            S.barrier()
            phv.close()
            if self.stop == "diff":
                return
            sbank_cfg["banks"] = [0, 1, 2, 5, 7]
            sbank_cfg["la"] = 2
            SVt = self.sb(ph, "SVt", [128, NKT, 130], BF16)
            S.dma("sp", SVt[:, :, :], s["SV"][:, :, :], writes=["SVt"])
            QG = [self.sb(ph, "QG", [128, 4, TA], BF16) for _ in range(2)]
            S.dma("sp", KT[0][:, :], s["SK"][:, :], writes=[("KT", 0)])
            for g in range(2):
                S.op("pool", lambda: nc.gpsimd.memset(QG[g][:], 0.0), writes=[("QG", g)])
                S.dma("sp", QG[g][g * 64:(g + 1) * 64, :, :],
                      s["SQ"][g * 256:(g + 1) * 256, :].rearrange("(i e) t -> e i t", e=64), writes=[("QG", g)])
            for g in range(2):
                qtiles = []
                if not self.last:
                    qtiles += [(0, None), (128, None)]
                qtiles += [(NCTX + 128 * qi, qi) for qi in range(16)]
                for (col0, qi) in qtiles:
                    if qi is None:
                        tiles = [(0, None), (1, None)]
                    else:
                        kt = 2 + qi
                        prev = (kt - 1, 1) if qi >= 1 else (33, 0)
                        nxt = (kt + 1, 2) if qi <= 14 else (18, 3)
                        tiles = [(0, None), (1, None), prev, (kt, None), nxt]
                    n = len(tiles)
                    ai = 4 + 2 * (accsel[0] % 2)
                    accsel[0] += 1
                    acc, acck = self.ps[ai], ("ps", ai)

                    def qk_fn(i, psb, psk, tiles=tiles, col0=col0, g=g):
                        kt, mi = tiles[i]
                        S.op("pe", lambda: nc.tensor.matmul(psb[:, :], KT[0][:, kt * 128:(kt + 1) * 128],
                                                            QG[g][:, :, col0:col0 + 128], start=True, stop=(mi is None)),
                             reads=[("KT", 0), ("QG", g)], writes=[psk])
                        if mi is not None:
                            S.op("pe", lambda: nc.tensor.matmul(psb[:, :], self.ident[:], maskb[:, mi, :], start=False, stop=True),
                                 reads=["consts", "maskb"], writes=[psk])

                    def pv_fn(i, pt, ptk, tiles=tiles, g=g, acc=acc, acck=acck, n=n):
                        kt, mi = tiles[i]
                        S.op("pe", lambda: nc.tensor.matmul(acc[0:65, :], SVt[:, kt, g * 65:(g + 1) * 65], pt[:, :],
                                                            start=(i == 0), stop=(i == n - 1)),
                             reads=[ptk, "SVt"], writes=[acck])

                    def fin(col0=col0, g=g, acc=acc, acck=acck):
                        bcast_recip(512, lambda: S.op("dve", lambda: nc.vector.tensor_tensor(
                            out=rec[R, :], in0=acc[R, :], in1=sinkrow[R, g, :], op=ALU.add),
                            reads=[acck, "sinkrow"], writes=["rec"]), [acck])
                        og, ogk = next_ostg()
                        S.op("dve", lambda: nc.vector.tensor_tensor(out=og[0:64, :], in0=acc[0:64, :], in1=bcsb[0][0:64, :],
                                                                    op=ALU.mult),
                             reads=[acck, ("bcsb", 0)], writes=[ogk])
                        S.dma("pool", s["OC"][g * 256:(g + 1) * 256, col0:col0 + 128].rearrange("(i e) q -> e i q", e=64),
                              og[0:64, :].rearrange("e (i q) -> e i q", q=128), reads=[ogk])
                    attn_block(n, 512, qk_fn, DH_SCALE, pv_fn, fin)
                flush()

    def phase_merge(self, l):
        nc, S, d, s = self.nc, self.S, self.d, self.s
        with ExitStack() as ph:
            wbr = self.sb(ph, "wbr", [128, 3, 4, 1024], BF16)
            wo = self.sb(ph, "wo", [128, KC, 1024], BF16)
            ot = [self.sb(ph, "ot", [128, 12, 512], BF16) for _ in range(2)]
            gt = [self.sb(ph, "gt", [128, 3, 512], BF16) for _ in range(3)]
            mt = self.sb(ph, "mt", [128, KC, 512], BF16)
            yf = self.sb(ph, "yf", [128, KC, 512], F32)
            tm = [self.sb(ph, "tm", [128, 512], F32) for _ in range(6)]
            sq = [self.sb(ph, "sq", [128, 512], BF16) for _ in range(2)]
            rstd = self.sb(ph, "rstd", [128, 512], F32)
            for br in range(3):
                S.dma("pool", wbr[:, br, :, :], d["w_br"][l, br], writes=["wbr"])
            S.dma("pool", wo[:], d["w_o"][l], writes=["wo"])
            onames = ("OA", "OB", "OC")
            cnt = {"gt": 0, "tm": 0, "sq": 0, "ps": 0}
            qbl = self.qblocks()

            def load_o(bi):
                c0, T, s_ = qbl[bi]
                o = ot[bi % 2]
                for br in range(3):
                    S.dma("sp", o[:, br * 4:(br + 1) * 4, :T],
                          s[onames[br]][:, c0:c0 + T].rearrange("(kc p) t -> p kc t", p=128), writes=[("ot", bi % 2)])
            load_o(0)
            for bi, (c0, T, s_) in enumerate(qbl):
                if bi + 1 < len(qbl):
                    load_o(bi + 1)
                o, ok = ot[bi % 2], ("ot", bi % 2)
                gview = s["GT"][:, c0:c0 + T].rearrange("(br oc p) t -> oc p br t", br=3, oc=8)
                for oc in range(KC):
                    gi = cnt["gt"] % 3
                    cnt["gt"] += 1
                    S.dma("sp", gt[gi][:, :, :T], gview[oc], writes=[("gt", gi)])
                    tms = []
                    for br in range(3):
                        pi = cnt["ps"] % 5
                        cnt["ps"] += 1
                        psb, psk = self.ps[pi], ("ps", pi)
                        for kc in range(4):
                            S.op("pe", lambda: nc.tensor.matmul(psb[:, :T], wbr[:, br, kc, oc * 128:(oc + 1) * 128],
                                                                o[:, br * 4 + kc, :T], start=(kc == 0), stop=(kc == 3)),
                                 reads=["wbr", ok], writes=[psk])
                        ti = cnt["tm"] % 6
                        cnt["tm"] += 1
                        S.op("dve", lambda: nc.vector.tensor_tensor(out=tm[ti][:, :T], in0=psb[:, :T], in1=gt[gi][:, br, :T],
                                                                    op=ALU.mult),
                             reads=[psk, ("gt", gi)], writes=[("tm", ti)])
                        tms.append(ti)
                    S.op("pool", lambda: nc.gpsimd.tensor_tensor(out=tm[tms[0]][:, :T], in0=tm[tms[0]][:, :T], in1=tm[tms[1]][:, :T],
                                                                 op=ALU.add),
                         reads=[("tm", tms[0]), ("tm", tms[1])], writes=[("tm", tms[0])])
                    S.op("pool", lambda: nc.gpsimd.tensor_tensor(out=mt[:, oc, :T], in0=tm[tms[0]][:, :T], in1=tm[tms[2]][:, :T],
                                                                 op=ALU.add),
                         reads=[("tm", tms[0]), ("tm", tms[2])], writes=[("mt", oc)])
                pst = self.ps[7]
                pend_stat = None
                for oc in range(KC):
                    pi = cnt["ps"] % 5
                    cnt["ps"] += 1
                    psb, psk = self.ps[pi], ("ps", pi)
                    for kc in range(KC):
                        S.op("pe", lambda: nc.tensor.matmul(psb[:, :T], wo[:, kc, oc * 128:(oc + 1) * 128], mt[:, kc, :T],
                                                            start=(kc == 0), stop=(kc == KC - 1)),
                             reads=["wo"] + [("mt", k) for k in range(KC)], writes=[psk])
                    if pend_stat is not None:
                        pend_stat()
                    S.op("act", lambda: nc.scalar.copy(out=yf[:, oc, :T], in_=psb[:, :T]), reads=[psk], writes=[("yf", oc)])
                    si = cnt["sq"] % 2
                    cnt["sq"] += 1
                    S.op("act", lambda: nc.scalar.activation(out=sq[si][:, :T], in_=psb[:, :T], func=AF.Square),
                         reads=[psk], writes=[("sq", si)])

                    def pend_stat(oc=oc, si=si, T=T):
                        S.op("pe", lambda: nc.tensor.matmul(pst[:, :T], self.onesD[:], sq[si][:, :T],
                                                            start=(oc == 0), stop=(oc == KC - 1)),
                             reads=[("sq", si), "consts"], writes=["ps7"])
                pend_stat()
                self.post_norm_residual(pst, rstd, yf, [("yf", k) for k in range(KC)], lambda oc: yf[:, oc, :T],
                                        c0, T, s_, 2, tm, cnt)

    def post_norm_residual(self, pst, rstd, yf, yfkeys, ysrc, c0, T, s_, gj, tm, cnt):
        nc, S = self.nc, self.S
        S.op("act", lambda: nc.scalar.activation(out=rstd[:, :T], in_=pst[:, :T], func=AF.Sqrt, bias=EPS, scale=1.0),
             reads=["ps7"], writes=["rstd"])
        S.op("dve", lambda: nc.vector.reciprocal(out=rstd[:, :T], in_=rstd[:, :T]), reads=["rstd"], writes=["rstd"])
        for oc in range(KC):
            ti = cnt["tm"] % len(tm)
            cnt["tm"] += 1
            S.op("dve", lambda: nc.vector.tensor_tensor(out=tm[ti][:, :T], in0=ysrc(oc), in1=rstd[:, :T], op=ALU.mult),
                 reads=[yfkeys[oc], "rstd"], writes=[("tm", ti)])
            S.op("dve", lambda: nc.vector.scalar_tensor_tensor(
                out=self.X[:, oc, c0:c0 + T], in0=tm[ti][:, :T], scalar=self.DER[:, s_, gj, oc:oc + 1],
                in1=self.X[:, oc, c0:c0 + T], op0=ALU.mult, op1=ALU.add),
                reads=[("tm", ti), "DER", "X"], writes=["X"])

    def phase_ffn(self, l):
        nc, S, d = self.nc, self.S, self.d
        if self.last:
            passes = [[(NCTX, 512, 0), (NCTX + 512, 512, 0)], [(NCTX + 1024, 512, 0), (NCTX + 1536, 512, 0)]]
        else:
            passes = [[(0, 256, 1), (256, 512, 0), (768, 384, 0)], [(1152, 512, 0), (1664, 512, 0), (2176, 128, 0)]]
        for blocks in passes:
            Tp = sum(b[1] for b in blocks)
            offs = []
            o = 0
            for b in blocks:
                offs.append(o)
                o += b[1]
            with ExitStack() as ph:
                a = self.sb(ph, "a", [128, NFC, Tp], BF16)
                with ExitStack() as ph1:
                    hf = self.sb(ph1, "hf", [128, KC, Tp], BF16)
                    tl = {"sq": [self.sb(ph1, "sq", [128, 512], BF16) for _ in range(2)], "sqi": 0,
                          "t": [self.sb(ph1, "t", [128, 512], F32) for _ in range(4)], "ti": 0,
                          "rstd": self.sb(ph1, "rstd", [128, 512], F32)}
                    wgu = [self.sb(ph1, "wgu", [128, KC, 256], BF16) for _ in range(3)]
                    sg = [self.sb(ph1, "sg", [128, 512], F32) for _ in range(3)]
                    for bi, (c0, T, s_) in enumerate(blocks):
                        of = offs[bi]
                        self.emit_norm_h(tl, lambda kc: self.X[:, kc, c0:c0 + T], ["X"], T,
                                         lambda kc: hf[:, kc, of:of + T], [("hf", bi)], s_, 3)
                    cnt = {"ps": 0, "sg": 0}
                    for fc in range(NFC):
                        w, wk = wgu[fc % 3], ("wgu", fc % 3)
                        S.dma("pool", w[:], d["w_gu"][l, fc], writes=[wk])
                        for bi, (c0, T, s_) in enumerate(blocks):
                            of = offs[bi]
                            pg, pgk = self.ps[cnt["ps"] % 6], ("ps", cnt["ps"] % 6)
                            cnt["ps"] += 1
                            pu, puk = self.ps[cnt["ps"] % 6], ("ps", cnt["ps"] % 6)
                            cnt["ps"] += 1
                            for kc in range(KC):
                                S.op("pe", lambda: nc.tensor.matmul(pg[:, :T], w[:, kc, 0:128], hf[:, kc, of:of + T],
                                                                    start=(kc == 0), stop=(kc == KC - 1)),
                                     reads=[wk, ("hf", bi)], writes=[pgk])
                            for kc in range(KC):
                                S.op("pe", lambda: nc.tensor.matmul(pu[:, :T], w[:, kc, 128:256], hf[:, kc, of:of + T],
                                                                    start=(kc == 0), stop=(kc == KC - 1)),
                                     reads=[wk, ("hf", bi)], writes=[puk])
                            si = cnt["sg"] % 3
                            cnt["sg"] += 1
                            S.op("act", lambda: nc.scalar.activation(out=sg[si][:, :T], in_=pg[:, :T], func=AF.Silu),
                                 reads=[pgk], writes=[("sg", si)])
                            S.op("dve", lambda: nc.vector.tensor_tensor(out=a[:, fc, of:of + T], in0=pu[:, :T], in1=sg[si][:, :T],
                                                                        op=ALU.mult),
                                 reads=[puk, ("sg", si)], writes=[("a", bi)])
                    S.barrier()
                if self.stop == "ffn_gu":
                    return
                with ExitStack() as ph2:
                    yf = self.sb(ph2, "yf", [128, KC, Tp], F32)
                    wdn = [self.sb(ph2, "wdn", [128, NFC, 128], BF16) for _ in range(2)]
                    sq = [self.sb(ph2, "sq", [128, 512], BF16) for _ in range(2)]
                    tm = [self.sb(ph2, "tm", [128, 512], F32) for _ in range(4)]
                    rstd = self.sb(ph2, "rstd", [128, 512], F32)
                    cnt = {"ps": 0, "tm": 0, "sq": 0}
                    for oc in range(KC):
                        w, wk = wdn[oc % 2], ("wdn", oc % 2)
                        S.dma("pool", w[:], d["w_dn"][l, oc], writes=[wk])
                        for bi, (c0, T, s_) in enumerate(blocks):
                            of = offs[bi]
                            psb, psk = self.ps[cnt["ps"] % 6], ("ps", cnt["ps"] % 6)
                            cnt["ps"] += 1
                            for kc in range(NFC):
                                S.op("pe", lambda: nc.tensor.matmul(psb[:, :T], w[:, kc, :], a[:, kc, of:of + T],
                                                                    start=(kc == 0), stop=(kc == NFC - 1)),
                                     reads=[wk, ("a", bi)], writes=[psk])
                            S.op("act", lambda: nc.scalar.copy(out=yf[:, oc, of:of + T], in_=psb[:, :T]),
                                 reads=[psk], writes=[("yf", bi, oc)])
                    pst = self.ps[7]
                    for bi, (c0, T, s_) in enumerate(blocks):
                        of = offs[bi]
                        for oc in range(KC):
                            si = cnt["sq"] % 2
                            cnt["sq"] += 1
                            S.op("act", lambda: nc.scalar.activation(out=sq[si][:, :T], in_=yf[:, oc, of:of + T], func=AF.Square),
                                 reads=[("yf", bi, oc)], writes=[("sq", si)])
                            S.op("pe", lambda: nc.tensor.matmul(pst[:, :T], self.onesD[:], sq[si][:, :T],
                                                                start=(oc == 0), stop=(oc == KC - 1)),
                                 reads=[("sq", si), "consts"], writes=["ps7"])
                        self.post_norm_residual(pst, rstd, yf, [("yf", bi, k) for k in range(KC)],
                                                lambda oc: yf[:, oc, of:of + T], c0, T, s_, 5, tm, cnt)
                    S.barrier()
                if self.stop == "ffn_p1":
                    return


_CACHE = {}


def _get_prog(layers, dbg=False):
    key = (tuple(layers), dbg)
    if key not in _CACHE:
        p = Prog(list(layers), dbg)
        p.build()
        _CACHE[key] = p
    return _CACHE[key]


def _core_inputs(x, ctx, c, c_ctx, wts, consts):
    maps = []
    for r in range(8):
        b, half = r // 2, r % 2
        own = x[b, half * SOWN:(half + 1) * SOWN]
        oth = x[b, (1 - half) * SOWN:(2 - half) * SOWN]
        m = {"xa": _fm(np.concatenate([ctx[b], own], axis=0)),
             "xo": _fm(oth),
             "cT": _fm(np.stack([c[b], c_ctx], axis=0))}
        m.update(wts)
        m.update(consts[half])
        m["sel"] = np.ascontiguousarray(np.tile(np.array([[1.0, 0.0]] if half == 1 else [[0.0, 1.0]], np.float32), (128, 1)))
        maps.append(m)
    return maps


def _gather(results, x, ctx):
    x = x.copy()
    ctx = ctx.copy()
    for r in range(8):
        b, half = r // 2, r % 2
        xo = results[r]["xout"]
        full = xo.transpose(2, 1, 0).reshape(TA, D)
        x[b, half * SOWN:(half + 1) * SOWN] = full[NCTX:]
        if half == 0:
            ctx[b] = full[:NCTX]
    return x, ctx


def kernel(**inputs):
    x = np.asarray(inputs["x"], np.float32)
    c = np.asarray(inputs["c"], np.float32)
    ctx = np.asarray(inputs["ctx"], np.float32)
    c_ctx = np.asarray(inputs["c_ctx"], np.float32)
    wts = _prep_weights(inputs)
    consts = [_prep_core_consts(0), _prep_core_consts(1)]
    prog = _get_prog(list(range(DEPTH)))
    maps = _core_inputs(x, ctx, c, c_ctx, wts, consts)
    res = run_bass_kernel_spmd(prog.nc, maps, core_ids=list(range(8)))
    x, ctx = _gather(res.results, x, ctx)
    return x
```

```python
from contextlib import ExitStack
import math
import numpy as np
import concourse.bass as bass
import concourse.mybir as mybir
from concourse.bass_utils import run_bass_kernel_spmd

F32 = mybir.dt.float32
BF16 = mybir.dt.bfloat16
AF = mybir.ActivationFunctionType
ALU = mybir.AluOpType
AX = mybir.AxisListType

D = 1024
KC = 8
NCTX = 256
SOWN = 2048
TA = NCTX + SOWN
TB = 2048
NK = TA + TB
NKT = NK // 128
EPS = 1e-6
MLA_SCALE = 96 ** -0.5
DH_SCALE = 0.125
NCH = 61
DFF = 2816
NFC = 22
DEPTH = 2

CH_CKV = (0, 1)
CH_KR, CH_KRS = 2, 3
CH_DK = (4, 5, 6, 7)
CH_DKS = (8, 9, 10, 11)
CH_DV = (12, 13, 14, 15)
CH_SK, CH_SKS, CH_SV = 16, 17, 18
CH_CQ = (19, 20)
CH_DQ = (21, 22, 23, 24)
CH_DQS = (25, 26, 27, 28)
CH_SQ = (29, 30, 31, 32)
CH_SQS = (33, 34, 35, 36)
CH_G0 = 37


class Sched:
    SEM_LIMIT = 30000
    NDMA = 10

    def __init__(self, nc, stack):
        self.nc = nc
        self.stack = stack
        self.eng = {"pe": nc.tensor, "act": nc.scalar, "dve": nc.vector,
                    "pool": nc.gpsimd, "sp": nc.sync}
        self.nsem = 0
        self.cur = {}
        for e in ("pe", "act", "dve", "pool"):
            self.cur[e] = [self._newsem(e), 0]
        self.dq = {q: {"sems": [], "vals": [], "i": 0} for q in ("sp", "pool", "act")}
        self.waited = {}
        self.state = {}
        self.persist = set()
        self.n_wait = 0
        self.n_ins = 0

    def _newsem(self, name):
        self.nsem += 1
        return self.stack.enter_context(self.nc.semaphore(f"s_{name}_{self.nsem}"))

    def _wait(self, e, deps):
        best = {}
        for (s, v) in deps:
            k = id(s)
            if k not in best or best[k][1] < v:
                best[k] = (s, v)
        for k, (s, v) in best.items():
            if e == "pe" and s is self.cur["pe"][0]:
                continue
            if self.waited.get((e, k), 0) >= v:
                continue
            self.eng[e].wait_ge(s, v)
            self.waited[(e, k)] = v
            self.n_wait += 1

    def _deps(self, reads, writes):
        deps = []
        for k in reads:
            st = self.state.get(k)
            if st is not None and st[0] is not None:
                deps.append(st[0])
        for k in writes:
            st = self.state.get(k)
            if st is not None:
                if st[0] is not None:
                    deps.append(st[0])
                deps.extend(st[1].values())
        return deps

    def _update(self, tok, reads, writes):
        sid = id(tok[0])
        for k in reads:
            st = self.state.setdefault(k, [None, {}])
            old = st[1].get(sid)
            if old is None or old[1] < tok[1]:
                st[1][sid] = tok
        for k in writes:
            self.state[k] = [tok, {}]

    def op(self, e, fn, reads=(), writes=()):
        self._wait(e, self._deps(reads, writes))
        ins = fn()
        c = self.cur[e]
        c[1] += 1
        ins.then_inc(c[0], 1)
        tok = (c[0], c[1])
        if c[1] >= self.SEM_LIMIT:
            self.cur[e] = [self._newsem(e), 0]
        self._update(tok, reads, writes)
        self.n_ins += 1
        return tok

    def dma(self, q, out, in_, reads=(), writes=()):
        d = self.dq[q]
        if len(d["sems"]) < self.NDMA:
            d["sems"].append(self._newsem("dma" + q))
            d["vals"].append(0)
        i = d["i"] % self.NDMA
        d["i"] += 1
        if d["vals"][i] + 16 > self.SEM_LIMIT:
            d["sems"][i] = self._newsem("dma" + q)
            d["vals"][i] = 0
        s, v = d["sems"][i], d["vals"][i]
        deps = self._deps(reads, writes)
        if v > 0:
            deps.append((s, v))
        self._wait(q, deps)
        self.eng[q].dma_start(out=out, in_=in_).then_inc(s, 16)
        d["vals"][i] = v + 16
        tok = (s, v + 16)
        self._update(tok, reads, writes)
        self.n_ins += 1
        return tok

    def all_tokens(self):
        toks = []
        for e, c in self.cur.items():
            if c[1] > 0:
                toks.append((c[0], c[1]))
        for q, d in self.dq.items():
            for s, v in zip(d["sems"], d["vals"]):
                if v > 0:
                    toks.append((s, v))
        return toks

    def barrier(self, engines=("pe", "act", "dve", "pool", "sp")):
        toks = self.all_tokens()
        for e in engines:
            self._wait(e, toks)
        self.state = {k: v for k, v in self.state.items() if k in self.persist}

    def collective(self, fn, reads=(), writes=()):
        self._wait("pool", self._deps(reads, writes))
        sem = self._newsem("cc")
        fn().then_inc(sem, 1)
        tok = (sem, 1)
        self._update(tok, reads, writes)
        for k in writes:
            self.persist.add(k)
        return tok


def _fm(a):
    T = a.shape[0]
    return np.ascontiguousarray(a.T.reshape(KC, 128, T).transpose(1, 0, 2))


def _partner(d):
    q = d // 4
    idx = np.arange(d)
    return np.where((idx % (2 * q)) < q, idx + q, idx - q)


def _swap_heads(cols, d):
    n = cols.shape[1] // d
    p = _partner(d)
    full = np.concatenate([h * d + p for h in range(n)])
    return cols[:, full]


def _chunk(cols):
    K, w = cols.shape
    out = np.zeros((K, 128), np.float32)
    out[:, :w] = cols
    return out.reshape(K // 128, 128, 128).transpose(1, 0, 2)


def _rope_tables(pos, d):
    n = pos.shape[0]
    q = d // 4
    axis_dim = d // 2
    inv = (10000.0 ** (-np.arange(0, axis_dim, 2, dtype=np.float32) / axis_dim)).astype(np.float32)
    valid = pos >= 0
    p = np.where(valid, pos, 0)
    r = (p // 64).astype(np.float32)
    c = (p % 64).astype(np.float32)
    ang_r = r[None, :] * inv[:, None]
    ang_c = c[None, :] * inv[:, None]
    C = np.zeros((d, n), np.float32)
    S = np.zeros((d, n), np.float32)
    C[0:q] = np.cos(ang_r); C[q:2 * q] = np.cos(ang_r)
    C[2 * q:3 * q] = np.cos(ang_c); C[3 * q:] = np.cos(ang_c)
    S[0:q] = -np.sin(ang_r); S[q:2 * q] = np.sin(ang_r)
    S[2 * q:3 * q] = -np.sin(ang_c); S[3 * q:] = np.sin(ang_c)
    C[:, ~valid] = 1.0
    S[:, ~valid] = 0.0
    return C, S


def _prep_weights(inp):
    L = DEPTH
    w = {}
    w_ada = np.asarray(inp["w_ada"], np.float32)
    w["w_ada"] = np.ascontiguousarray(
        w_ada.reshape(L, KC, 128, 12, 512).transpose(0, 3, 2, 1, 4))
    w["b_ada"] = np.ascontiguousarray(
        np.asarray(inp["b_ada"], np.float32).reshape(L, 48, 128).transpose(0, 2, 1))
    nws = np.stack([np.asarray(inp[k], np.float32) for k in
                    ("attn_pre_norm", "attn_post_norm", "ffn_pre_norm", "ffn_post_norm")], axis=1)
    w["nw"] = np.ascontiguousarray(nws.reshape(L, 4, KC, 128).transpose(0, 3, 1, 2))
    w_in = np.asarray(inp["w_in"], np.float32)
    chunks_all = []
    for l in range(L):
        W = w_in[l]
        kv, qq = W[:, :1568], W[:, 1568:]
        ch = []
        ch += [kv[:, 0:128], kv[:, 128:256]]
        kr = kv[:, 256:288]
        ch += [kr, _swap_heads(kr, 32)]
        dk = kv[:, 288:800]
        dks = _swap_heads(dk, 64)
        ch += [dk[:, i * 128:(i + 1) * 128] for i in range(4)]
        ch += [dks[:, i * 128:(i + 1) * 128] for i in range(4)]
        dv = kv[:, 800:1312]
        ch += [dv[:, i * 128:(i + 1) * 128] for i in range(4)]
        sk = kv[:, 1312:1440]
        ch += [sk, _swap_heads(sk, 64), kv[:, 1440:1568]]
        ch += [qq[:, 0:128], qq[:, 128:256]]
        dq = qq[:, 256:768]
        dqs = _swap_heads(dq, 64)
        ch += [dq[:, i * 128:(i + 1) * 128] for i in range(4)]
        ch += [dqs[:, i * 128:(i + 1) * 128] for i in range(4)]
        sq = qq[:, 768:1280]
        sqs = _swap_heads(sq, 64)
        ch += [sq[:, i * 128:(i + 1) * 128] for i in range(4)]
        ch += [sqs[:, i * 128:(i + 1) * 128] for i in range(4)]
        g = qq[:, 1280:]
        ch += [g[:, i * 128:(i + 1) * 128] for i in range(24)]
        assert len(ch) == NCH
        chunks_all.append(np.stack([_chunk(c) for c in ch]))
    w["w_in"] = np.ascontiguousarray(np.stack(chunks_all))
    w_ukv = np.asarray(inp["w_ukv"], np.float32).reshape(L, 2, 128, 8, 128)
    nope = w_ukv[..., 0:64].reshape(L, 2, 128, 4, 128)
    w["w_ukn"] = np.ascontiguousarray(nope.transpose(0, 2, 3, 1, 4))
    vv = w_ukv[..., 64:128].reshape(L, 2, 128, 512)
    w["w_ukv"] = np.ascontiguousarray(vv.transpose(0, 2, 1, 3))
    w_uq = np.asarray(inp["w_uq"], np.float32)
    uq = w_uq.reshape(L, 2, 128, 8, 96)
    w["w_uq"] = np.ascontiguousarray(uq.transpose(0, 2, 3, 1, 4))
    uqs = uq.copy()
    uqs[..., 64:96] = uq[..., 64:96][..., _partner(32)]
    w["w_uqs"] = np.ascontiguousarray(uqs.transpose(0, 2, 3, 1, 4))
    w["qn"] = np.ascontiguousarray(np.asarray(inp["mla_q_norm"], np.float32).reshape(L, 2, 128).transpose(0, 2, 1))
    w["kvn"] = np.ascontiguousarray(np.asarray(inp["mla_kv_norm"], np.float32).reshape(L, 2, 128).transpose(0, 2, 1))
    w["dlam"] = np.ascontiguousarray(np.asarray(inp["diff_lambda"], np.float32).reshape(L, 1, 256))
    w["dnorm"] = np.ascontiguousarray(np.asarray(inp["diff_norm"], np.float32).reshape(L, 2, 64).transpose(0, 2, 1))
    w["sink"] = np.ascontiguousarray(np.asarray(inp["swa_sink"], np.float32).reshape(L, 1, 8))
    br = np.stack([np.asarray(inp[k], np.float32) for k in ("w_branch_a", "w_branch_b", "w_branch_c")], axis=1)
    w["w_br"] = np.ascontiguousarray(br.reshape(L, 3, 4, 128, 1024).transpose(0, 1, 3, 2, 4))
    w["w_o"] = np.ascontiguousarray(
        np.asarray(inp["w_o"], np.float32).reshape(L, KC, 128, 1024).transpose(0, 2, 1, 3))
    gu = np.asarray(inp["w_gate_up"], np.float32)
    g_ = gu[:, :, :DFF].reshape(L, KC, 128, NFC, 128)
    u_ = gu[:, :, DFF:].reshape(L, KC, 128, NFC, 128)
    w["w_gu"] = np.ascontiguousarray(np.concatenate([g_, u_], axis=-1).transpose(0, 3, 2, 1, 4))
    dn = np.asarray(inp["w_down"], np.float32).reshape(L, NFC, 128, KC, 128)
    w["w_dn"] = np.ascontiguousarray(dn.transpose(0, 3, 2, 1, 4))
    w["ident"] = np.eye(128, dtype=np.float32)
    return w


def _prep_core_consts(half):
    own = np.arange(half * SOWN, (half + 1) * SOWN)
    oth = np.arange((1 - half) * SOWN, (2 - half) * SOWN)
    pos = np.concatenate([-np.ones(NCTX, np.int64), own, oth])
    C64, S64 = _rope_tables(pos, 64)
    C32, S32 = _rope_tables(pos, 32)
    t64 = np.stack([np.concatenate([C64, C64]), np.concatenate([S64, S64])])
    z = np.zeros((32, NK), np.float32)
    t32 = np.stack([np.concatenate([C32, z, C32, z]), np.concatenate([S32, z, S32, z])])
    kk = np.arange(128)[:, None]
    qq = np.arange(128)[None, :]
    NEG = -30000.0
    prev = np.where(qq <= kk, 0.0, NEG).astype(np.float32)
    nxt = np.where(kk <= qq, 0.0, NEG).astype(np.float32)
    full = np.full((128, 128), NEG, np.float32)
    m = np.stack([full if half == 0 else prev, prev, nxt, nxt if half == 0 else full], axis=1)
    m = np.ascontiguousarray(np.repeat(m[:, :, None, :], 4, axis=2).reshape(128, 4, 512))
    return {"rope64": np.ascontiguousarray(t64), "rope32": np.ascontiguousarray(t32), "maskb": m}


class Prog:
    def __init__(self, layers, dbg=False, stop=None, stop_layer=None):
        self.stop_cfg = stop
        self.stop_layer = stop_layer
        self.stop = stop
        self.layers = layers
        self.dbg = dbg
        self.nc = bass.Bass("TRN2", target_bir_lowering=False)
        self.uid = 0

    def din(self, name, shape):
        return self.nc.dram_tensor(name, list(shape), F32, kind="ExternalInput").ap()

    def dscr(self, name, shape, dt=BF16):
        kind = "ExternalOutput" if self.dbg else "Internal"
        return self.nc.dram_tensor(name, list(shape), dt, kind=kind).ap()

    def sb(self, stack, name, shape, dt):
        self.uid += 1
        return stack.enter_context(self.nc.sbuf_tensor(f"{name}_{self.uid}", list(shape), dt))

    def build(self):
        nc = self.nc
        L = DEPTH
        d = {}
        d["xa"] = self.din("xa", [128, KC, TA])
        d["xo"] = self.din("xo", [128, KC, TB])
        d["cT"] = self.din("cT", [128, KC, 2])
        d["w_ada"] = self.din("w_ada", [L, 12, 128, KC, 512])
        d["b_ada"] = self.din("b_ada", [L, 128, 48])
        d["nw"] = self.din("nw", [L, 128, 4, KC])
        d["w_in"] = self.din("w_in", [L, NCH, 128, KC, 128])
        d["w_ukn"] = self.din("w_ukn", [L, 128, 4, 2, 128])
        d["w_ukv"] = self.din("w_ukv", [L, 128, 2, 512])
        d["w_uq"] = self.din("w_uq", [L, 128, 8, 2, 96])
        d["w_uqs"] = self.din("w_uqs", [L, 128, 8, 2, 96])
        d["qn"] = self.din("qn", [L, 128, 2])
        d["kvn"] = self.din("kvn", [L, 128, 2])
        d["dlam"] = self.din("dlam", [L, 1, 256])
        d["dnorm"] = self.din("dnorm", [L, 64, 2])
        d["sink"] = self.din("sink", [L, 1, 8])
        d["w_br"] = self.din("w_br", [L, 3, 128, 4, 1024])
        d["w_o"] = self.din("w_o", [L, 128, KC, 1024])
        d["w_gu"] = self.din("w_gu", [L, NFC, 128, KC, 256])
        d["w_dn"] = self.din("w_dn", [L, KC, 128, NFC, 128])
        d["ident"] = self.din("ident", [128, 128])
        d["rope64"] = self.din("rope64", [2, 128, NK])
        d["rope32"] = self.din("rope32", [2, 128, NK])
        d["maskb"] = self.din("maskb", [128, 4, 512])
        d["sel"] = self.din("sel", [128, 2])
        self.d = d
        self.gin = [nc.dram_tensor(f"gin{i}", [128 * 2, SOWN], F32, kind="Internal").ap() for i in range(4)]
        self.gout = [nc.dram_tensor(f"gout{i}", [2 * 128 * 2, SOWN], F32, kind="Internal").ap() for i in range(4)]
        self.out = nc.dram_tensor("xout", [128, KC, TA], F32, kind="ExternalOutput").ap()
        s = {}
        s["KN"] = self.dscr("KN", [512, NK])
        s["KR"] = self.dscr("KR", [32, NK])
        s["VM"] = self.dscr("VM", [128, NKT, 520])
        s["DK"] = self.dscr("DK", [512, NK])
        s["DV"] = self.dscr("DV", [128, NKT, 520])
        s["SK"] = self.dscr("SK", [128, NK])
        s["SV"] = self.dscr("SV", [128, NKT, 130])
        s["QM"] = self.dscr("QM", [8, 96, TA])
        s["DQ"] = self.dscr("DQ", [512, TA])
        s["SQ"] = self.dscr("SQ", [512, TA])
        s["GT"] = self.dscr("GT", [3072, TA])
        s["OA"] = self.dscr("OA", [512, TA])
        s["OB"] = self.dscr("OB", [512, TA])
        s["OC"] = self.dscr("OC", [512, TA])
        self.s = s

        with ExitStack() as st:
            self.S = S = Sched(nc, st)
            self.ps = [st.enter_context(nc.psum_tensor(f"ps{i}", [128, 512], F32)) for i in range(8)]
            self.X = self.sb(st, "X", [128, KC, TA], F32)
            self.onesD = self.sb(st, "onesD", [128, 128], BF16)
            self.ones256 = self.sb(st, "ones256", [128, 128], BF16)
            self.ones128 = self.sb(st, "ones128", [128, 128], BF16)
            self.onesb = self.sb(st, "onesb", [128, 128], BF16)
            self.onesf = self.sb(st, "onesf", [128, 128], F32)
            self.zrow = self.sb(st, "zrow", [128, 128], F32)
            self.ident = self.sb(st, "ident", [128, 128], BF16)
            self.DER = self.sb(st, "DER", [128, 2, 6, KC], F32)
            self.MOD = self.sb(st, "MOD", [128, 2, 48], F32)
            self.small = self.sb(st, "small", [128, 256], F32)
            for t, v in ((self.onesD, 1.0 / 1024), (self.ones256, 1.0 / 256), (self.ones128, 1.0 / 128),
                         (self.onesb, 1.0), (self.onesf, 1.0), (self.zrow, 0.0)):
                S.op("dve", lambda t=t, v=v: nc.vector.memset(t[:], v), writes=["consts"])
            S.dma("pool", self.ident[:], d["ident"], writes=["consts"])
            S.dma("sp", self.X[:], d["xa"], writes=["X"])
            self.sel = self.sb(st, "sel", [128, 2], F32)
            S.dma("sp", self.sel[:], d["sel"], writes=["consts"])
            for li, l in enumerate(self.layers):
                last = (l == DEPTH - 1)
                self.gathered = (li > 0)
                if li > 0:
                    self.exchange_publish()
                self.layer(l, last)
            S.barrier()
            if self.dbg:
                dd = nc.dram_tensor("dbg_der", [128, 2, 6, KC], F32, kind="ExternalOutput").ap()
                S.dma("sp", dd, self.DER[:], reads=["DER"], writes=["dbgder"])
            S.dma("sp", self.out, self.X[:], reads=["X"], writes=["out"])
            S.barrier(engines=("sp",))
            self.stats = (S.n_ins, S.n_wait, S.nsem)
        return nc

    def exchange_publish(self):
        S = self.S
        S.barrier()
        for i in range(4):
            S.dma("sp", self.gin[i].rearrange("(p k) t -> p k t", k=2), self.X[:, 2 * i:2 * i + 2, NCTX:TA],
                  reads=["X"], writes=[("gin", i)])
            S.persist.add(("gin", i))

    def exchange_gather(self):
        nc, S = self.nc, self.S
        for i in range(4):
            S.collective(lambda: nc.gpsimd.collective_compute(
                "AllGather", ALU.bypass, replica_groups=[[0, 1], [2, 3], [4, 5], [6, 7]],
                ins=[self.gin[i]], outs=[self.gout[i]]), reads=[("gin", i)], writes=[("gout", i)])

    def layer(self, l, last):
        self.stop = self.stop_cfg if (self.stop_layer is None or self.stop_layer == l) else None
        self.l = l
        self.last = last
        S = self.S
        self.phase_ada(l)
        if self.gathered:
            self.exchange_gather()
        S.barrier()
        if self.stop == "ada":
            return
        self.phase_proj(l)
        S.barrier()
        if self.stop in ("proj", "projA"):
            return
        self.phase_attn(l)
        S.barrier()
        if self.stop in ("attn", "mla", "diff"):
            return
        self.phase_merge(l)
        S.barrier()
        if self.stop == "merge":
            return
        self.phase_ffn(l)
        S.barrier()

    def qblocks(self):
        blks = []
        if not self.last:
            blks.append((0, NCTX, 1))
        for i in range(4):
            blks.append((NCTX + 512 * i, 512, 0))
        return blks

    def phase_ada(self, l):
        nc, S, d = self.nc, self.S, self.d
        with ExitStack() as ph:
            wb = [self.sb(ph, "wada", [128, KC, 512], BF16) for _ in range(2)]
            cT = self.sb(ph, "cT", [128, KC, 2], F32)
            silu = self.sb(ph, "silu", [128, KC, 2], BF16)
            bada = self.sb(ph, "bada", [128, 48], F32)
            nw = self.sb(ph, "nw", [128, 4, KC], F32)
            S.dma("sp", cT[:], d["cT"], writes=["cT"])
            S.dma("sp", bada[:], d["b_ada"][l], writes=["bada"])
            S.dma("sp", nw[:], d["nw"][l], writes=["nw"])
            S.op("act", lambda: nc.scalar.activation(out=silu[:], in_=cT[:], func=AF.Silu),
                 reads=["cT"], writes=["silu"])
            psm = self.ps[0]
            for blk in range(12):
                w = wb[blk % 2]
                wk = ("wada", blk % 2)
                S.dma("pool", w[:], d["w_ada"][l, blk], writes=[wk])
                for j in range(4):
                    oc = blk * 4 + j
                    for kc in range(KC):
                        S.op("pe", lambda: nc.tensor.matmul(psm[:, oc * 2:oc * 2 + 2], w[:, kc, j * 128:(j + 1) * 128],
                                                            silu[:, kc, :], start=(kc == 0), stop=(kc == KC - 1)),
                             reads=[wk, "silu"], writes=["ps0"])
            pv = psm[:, 0:96].rearrange("p (a s) -> p a s", s=2)
            for s_ in range(2):
                S.op("dve", lambda: nc.vector.tensor_tensor(out=self.MOD[:, s_, :], in0=pv[:, :, s_], in1=bada[:],
                                                            op=ALU.add),
                     reads=["ps0", "bada"], writes=["MOD"])
            M, DER = self.MOD, self.DER
            for s_ in range(2):
                for (j, sc_i, sh_i, g_i, npre, npost) in ((0, 1, 0, 2, 0, 1), (3, 4, 3, 5, 2, 3)):
                    S.op("dve", lambda: nc.vector.scalar_tensor_tensor(
                        out=DER[:, s_, j, :], in0=M[:, s_, sc_i * 8:(sc_i + 1) * 8], scalar=1.0, in1=nw[:, npre, :],
                        op0=ALU.add, op1=ALU.mult), reads=["MOD", "nw"], writes=["DER"])
                    S.op("dve", lambda: nc.vector.tensor_copy(out=DER[:, s_, j + 1, :], in_=M[:, s_, sh_i * 8:(sh_i + 1) * 8]),
                         reads=["MOD"], writes=["DER"])
                    S.op("dve", lambda: nc.vector.tensor_tensor(out=DER[:, s_, j + 2, :], in0=M[:, s_, g_i * 8:(g_i + 1) * 8],
                                                                in1=nw[:, npost, :], op=ALU.mult),
                         reads=["MOD", "nw"], writes=["DER"])
            S.barrier()

    def emit_norm_h(self, tl, src, src_keys, T, dst, dst_keys, s_, j):
        nc, S = self.nc, self.S
        pst = self.ps[7]
        for kc in range(KC):
            sq, sqk = tl["sq"][tl["sqi"] % len(tl["sq"])], ("sq", tl["sqi"] % len(tl["sq"]))
            tl["sqi"] += 1
            S.op("act", lambda: nc.scalar.activation(out=sq[:, :T], in_=src(kc), func=AF.Square),
                 reads=src_keys, writes=[sqk])
            S.op("pe", lambda: nc.tensor.matmul(pst[:, :T], self.onesD[:], sq[:, :T], start=(kc == 0), stop=(kc == KC - 1)),
                 reads=[sqk, "consts"], writes=["ps7"])
        rstd = tl["rstd"]
        S.op("act", lambda: nc.scalar.activation(out=rstd[:, :T], in_=pst[:, :T], func=AF.Sqrt, bias=EPS, scale=1.0),
             reads=["ps7"], writes=["rstd"])
        S.op("dve", lambda: nc.vector.reciprocal(out=rstd[:, :T], in_=rstd[:, :T]), reads=["rstd"], writes=["rstd"])
        for kc in range(KC):
            t, tk = tl["t"][tl["ti"] % len(tl["t"])], ("t", tl["ti"] % len(tl["t"]))
            tl["ti"] += 1
            S.op("dve", lambda: nc.vector.scalar_tensor_tensor(
                out=t[:, :T], in0=src(kc), scalar=self.DER[:, s_, j, kc:kc + 1], in1=rstd[:, :T],
                op0=ALU.mult, op1=ALU.mult), reads=list(src_keys) + ["rstd", "DER"], writes=[tk])
            S.op("act", lambda: nc.scalar.activation(out=dst(kc), in_=t[:, :T], func=AF.Identity,
                                                     bias=self.DER[:, s_, j + 1, kc:kc + 1], scale=1.0),
                 reads=[tk, "DER"], writes=dst_keys)

    def phase_proj(self, l):
        nc, S, d, s = self.nc, self.S, self.d, self.s
        with ExitStack() as ph:
            hT = self.sb(ph, "hT", [128, KC, TA], BF16)
            xob = self.sb(ph, "xob", [128, KC, 512], F32)
            tl = {"sq": [self.sb(ph, "sq", [128, 512], BF16) for _ in range(2)], "sqi": 0,
                  "t": [self.sb(ph, "t", [128, 512], F32) for _ in range(4)], "ti": 0,
                  "rstd": self.sb(ph, "rstd", [128, 512], F32)}
            wbuf = [self.sb(ph, "wbuf", [128, KC, 128], BF16) for _ in range(4)]
            wdv = self.sb(ph, "wdv", [128, 4, KC, 128], BF16)
            wukn = self.sb(ph, "wukn", [128, 4, 2, 128], BF16)
            wukv = self.sb(ph, "wukv", [128, 2, 512], BF16)
            wuq = self.sb(ph, "wuq", [128, 8, 2, 96], BF16)
            wuqs = self.sb(ph, "wuqs", [128, 8, 2, 96], BF16)
            qn = self.sb(ph, "qn", [128, 2], F32)
            kvn = self.sb(ph, "kvn", [128, 2], F32)
            cf = self.sb(ph, "cf", [128, 2, 512], F32)
            cn = self.sb(ph, "cn", [128, 2, 512], BF16)
            tab = [self.sb(ph, "tab", [128, 2, 512], F32) for _ in range(3)]
            stg = [self.sb(ph, "stg", [128, 512], BF16) for _ in range(4)]
            vst = [self.sb(ph, "vst", [128, 8, 65], BF16) for _ in range(2)]
            svst = [self.sb(ph, "svst", [128, 2, 65], BF16) for _ in range(2)]
            st_ = {"tab": 0, "stg": 0, "vst": 0, "svst": 0, "ps": 0}

            for v in vst + svst:
                S.op("dve", lambda: nc.vector.memset(v[:], 1.0), writes=["vst_init"])
            S.dma("pool", wukn[:], d["w_ukn"][l], writes=["wsmall"])
            S.dma("pool", wukv[:], d["w_ukv"][l], writes=["wsmall"])
            S.dma("pool", wuq[:], d["w_uq"][l], writes=["wsmall"])
            S.dma("pool", wuqs[:], d["w_uqs"][l], writes=["wsmall"])
            S.dma("sp", qn[:], d["qn"][l], writes=["wsmall"])
            S.dma("sp", kvn[:], d["kvn"][l], writes=["wsmall"])

            def nextps():
                i = st_["ps"] % 6
                st_["ps"] += 1
                return self.ps[i], ("ps", i)

            def nextstg():
                i = st_["stg"] % len(stg)
                st_["stg"] += 1
                return stg[i], ("stg", i)

            class WStream:
                def __init__(ws, seq):
                    ws.seq = list(seq)
                    ws.issued = 0
                    ws.used = 0
                    ws.slots = {}

                def _issue(ws):
                    i = ws.issued
                    bi = i % len(wbuf)
                    S.dma("pool", wbuf[bi][:], d["w_in"][l, ws.seq[i]], writes=[("wbuf", bi)])
                    ws.slots[i] = bi
                    ws.issued += 1

                def get(ws, ci, n_live=1):
                    i = ws.used
                    assert ws.seq[i] == ci, (ws.seq[i], ci)
                    ws.used += 1
                    while ws.issued < len(ws.seq) and ws.issued <= i + 2:
                        ws._issue()
                    bi = ws.slots[i]
                    return wbuf[bi], ("wbuf", bi)

            def load_tab(name, kcol, T):
                i = st_["tab"] % len(tab)
                st_["tab"] += 1
                S.dma("act", tab[i][:, :, :T], d[name][:, :, kcol:kcol + T].rearrange("a p t -> p a t"),
                      writes=[("tab", i)])
                return tab[i], ("tab", i)

            def fm_group(psb, psk, wt, wk, M, blk):
                c0, T, bi = blk[0], blk[1], blk[3]
                for kc in range(KC):
                    S.op("pe", lambda: nc.tensor.matmul(psb[0:M, :T], wt[:, kc, 0:M], hT[:, kc, c0:c0 + T],
                                                        start=(kc == 0), stop=(kc == KC - 1)),
                         reads=[wk, ("hT", bi)], writes=[psk])

            def rope_epi(psz, pzk, pss, psk, p0, p1, T, tb, tbk, out_stg, ostk):
                t1, t1k = tl["t"][tl["ti"] % 4], ("t", tl["ti"] % 4)
                tl["ti"] += 1
                t2, t2k = tl["t"][tl["ti"] % 4], ("t", tl["ti"] % 4)
                tl["ti"] += 1
                S.op("dve", lambda: nc.vector.tensor_tensor(out=t1[p0:p1, :T], in0=psz[p0:p1, :T], in1=tb[p0:p1, 0, :T], op=ALU.mult),
                     reads=[pzk, tbk], writes=[t1k])
                S.op("dve", lambda: nc.vector.tensor_tensor(out=t2[p0:p1, :T], in0=pss[p0:p1, :T], in1=tb[p0:p1, 1, :T], op=ALU.mult),
                     reads=[psk, tbk], writes=[t2k])
                S.op("pool", lambda: nc.gpsimd.tensor_tensor(out=out_stg[p0:p1, :T], in0=t1[p0:p1, :T], in1=t2[p0:p1, :T], op=ALU.add),
                     reads=[t1k, t2k], writes=[ostk])

            def lowrank(chs, blk, normw):
                c0, T = blk[0], blk[1]
                pst = self.ps[7]
                banks = []
                for oc in range(2):
                    psb, psk = nextps()
                    fm_group(psb, psk, chs[oc][0], chs[oc][1], 128, blk)
                    banks.append((psb, psk))
                for oc in range(2):
                    psb, psk = banks[oc]
                    S.op("act", lambda: nc.scalar.copy(out=cf[:, oc, :T], in_=psb[:, :T]), reads=[psk], writes=[("cf", oc)])
                    sq, sqk = tl["sq"][tl["sqi"] % 2], ("sq", tl["sqi"] % 2)
                    tl["sqi"] += 1
                    S.op("act", lambda: nc.scalar.activation(out=sq[:, :T], in_=psb[:, :T], func=AF.Square),
                         reads=[psk], writes=[sqk])
                    S.op("pe", lambda: nc.tensor.matmul(pst[:, :T], self.ones256[:], sq[:, :T], start=(oc == 0), stop=(oc == 1)),
                         reads=[sqk, "consts"], writes=["ps7"])
                rstd = tl["rstd"]
                S.op("act", lambda: nc.scalar.activation(out=rstd[:, :T], in_=pst[:, :T], func=AF.Sqrt, bias=EPS, scale=1.0),
                     reads=["ps7"], writes=["rstd"])
                S.op("dve", lambda: nc.vector.reciprocal(out=rstd[:, :T], in_=rstd[:, :T]), reads=["rstd"], writes=["rstd"])
                for oc in range(2):
                    S.op("dve", lambda: nc.vector.scalar_tensor_tensor(
                        out=cn[:, oc, :T], in0=cf[:, oc, :T], scalar=normw[:, oc:oc + 1], in1=rstd[:, :T],
                        op0=ALU.mult, op1=ALU.mult), reads=[("cf", oc), "rstd", "wsmall"], writes=["cn"])

            KV_SEQ = [CH_CKV[0], CH_CKV[1], CH_KR, CH_KRS]
            for i in range(4):
                KV_SEQ += [CH_DK[i], CH_DKS[i]]
            KV_SEQ += [CH_SK, CH_SKS, CH_SV]
            Q_SEQ = [CH_CQ[0], CH_CQ[1]]
            for i in range(4):
                Q_SEQ += [CH_DQ[i], CH_DQS[i]]
            for i in range(4):
                Q_SEQ += [CH_SQ[i], CH_SQS[i]]
            Q_SEQ += [CH_G0 + i for i in range(24)]

            def kv_pass(ws, blocks, kbase):
                for i in range(4):
                    S.dma("pool", wdv[:, i, :, :], d["w_in"][l, CH_DV[i]], writes=["wdv"])
                chs = [ws.get(CH_CKV[0]), ws.get(CH_CKV[1])]
                for blk in blocks:
                    c0, T = blk[0], blk[1]
                    kcol = kbase + c0
                    lowrank(chs, blk, kvn)
                    for j in range(4):
                        psb, psk = nextps()
                        for k2 in range(2):
                            S.op("pe", lambda: nc.tensor.matmul(psb[:, :T], wukn[:, j, k2, :], cn[:, k2, :T],
                                                                start=(k2 == 0), stop=(k2 == 1)),
                                 reads=["wsmall", "cn"], writes=[psk])
                        sg, sgk = nextstg()
                        S.op("act", lambda: nc.scalar.copy(out=sg[:, :T], in_=psb[:, :T]), reads=[psk], writes=[sgk])
                        S.dma("sp", s["KN"][j * 128:(j + 1) * 128, kcol:kcol + T], sg[:, :T], reads=[sgk])
                    for tt in range(T // 128):
                        psb, psk = nextps()
                        for k2 in range(2):
                            S.op("pe", lambda: nc.tensor.matmul(psb[:, :], cn[:, k2, tt * 128:(tt + 1) * 128], wukv[:, k2, :],
                                                                start=(k2 == 0), stop=(k2 == 1)),
                                 reads=["wsmall", "cn"], writes=[psk])
                        vi = st_["vst"] % 2
                        st_["vst"] += 1
                        S.op("dve", lambda: nc.vector.tensor_copy(out=vst[vi][:, :, 0:64],
                                                                  in_=psb[:, :].rearrange("p (h e) -> p h e", e=64)),
                             reads=[psk, "vst_init"], writes=[("vst", vi)])
                        S.dma("sp", s["VM"][:, (kcol // 128) + tt, :], vst[vi][:].rearrange("p h e -> p (h e)"),
                              reads=[("vst", vi)])
                wz, wzk = ws.get(CH_KR)
                wsw, wswk = ws.get(CH_KRS)
                for blk in blocks:
                    c0, T = blk[0], blk[1]
                    kcol = kbase + c0
                    pa, pak = nextps()
                    fm_group(pa, pak, wz, wzk, 32, blk)
                    pb, pbk = nextps()
                    fm_group(pb, pbk, wsw, wswk, 32, blk)
                    tb, tbk = load_tab("rope32", kcol, T)
                    sg, sgk = nextstg()
                    rope_epi(pa, pak, pb, pbk, 0, 32, T, tb, tbk, sg, sgk)
                    S.dma("sp", s["KR"][:, kcol:kcol + T], sg[0:32, :T], reads=[sgk])
                for (cz, cs, dst, r0) in [(CH_DK[i], CH_DKS[i], "DK", i * 128) for i in range(4)] + [(CH_SK, CH_SKS, "SK", 0)]:
                    wz, wzk = ws.get(cz)
                    wsw, wswk = ws.get(cs)
                    for blk in blocks:
                        c0, T = blk[0], blk[1]
                        kcol = kbase + c0
                        pa, pak = nextps()
                        fm_group(pa, pak, wz, wzk, 128, blk)
                        pb, pbk = nextps()
                        fm_group(pb, pbk, wsw, wswk, 128, blk)
                        tb, tbk = load_tab("rope64", kcol, T)
                        sg, sgk = nextstg()
                        rope_epi(pa, pak, pb, pbk, 0, 128, T, tb, tbk, sg, sgk)
                        S.dma("sp", s[dst][r0:r0 + 128, kcol:kcol + T], sg[:, :T], reads=[sgk])
                wz, wzk = ws.get(CH_SV)
                for blk in blocks:
                    c0, T, bi = blk[0], blk[1], blk[3]
                    kcol = kbase + c0
                    for tt in range(T // 128):
                        psb, psk = nextps()
                        for kc in range(KC):
                            S.op("pe", lambda: nc.tensor.matmul(psb[:, 0:128], hT[:, kc, c0 + tt * 128:c0 + (tt + 1) * 128],
                                                                wz[:, kc, :], start=(kc == 0), stop=(kc == KC - 1)),
                                 reads=[wzk, ("hT", bi)], writes=[psk])
                        vi = st_["svst"] % 2
                        st_["svst"] += 1
                        S.op("dve", lambda: nc.vector.tensor_copy(out=svst[vi][:, :, 0:64],
                                                                  in_=psb[:, 0:128].rearrange("p (h e) -> p h e", e=64)),
                             reads=[psk, "vst_init"], writes=[("svst", vi)])
                        S.dma("sp", s["SV"][:, (kcol // 128) + tt, :], svst[vi][:].rearrange("p h e -> p (h e)"),
                              reads=[("svst", vi)])
                for blk in blocks:
                    c0, T, bi = blk[0], blk[1], blk[3]
                    kcol = kbase + c0
                    for tt in range(T // 128):
                        psb, psk = nextps()
                        for kc in range(KC):
                            S.op("pe", lambda: nc.tensor.matmul(psb[:, :], hT[:, kc, c0 + tt * 128:c0 + (tt + 1) * 128],
                                                                wdv[:, :, kc, :], start=(kc == 0), stop=(kc == KC - 1)),
                                 reads=["wdv", ("hT", bi)], writes=[psk])
                        vi = st_["vst"] % 2
                        st_["vst"] += 1
                        S.op("act", lambda: nc.scalar.copy(out=vst[vi][:, :, 0:64],
                                                           in_=psb[:, :].rearrange("p (h e) -> p h e", e=64)),
                             reads=[psk, "vst_init"], writes=[("vst", vi)])
                        S.dma("sp", s["DV"][:, (kcol // 128) + tt, :], vst[vi][:].rearrange("p h e -> p (h e)"),
                              reads=[("vst", vi)])

            def q_pass(ws, blocks):
                chs = [ws.get(CH_CQ[0]), ws.get(CH_CQ[1])]
                for blk in blocks:
                    c0, T = blk[0], blk[1]
                    lowrank(chs, blk, qn)
                    tb, tbk = load_tab("rope32", c0, T)
                    for h in range(8):
                        pa, pak = nextps()
                        pb, pbk = nextps()
                        for k2 in range(2):
                            S.op("pe", lambda: nc.tensor.matmul(pa[0:96, :T], wuq[:, h, k2, :], cn[:, k2, :T],
                                                                start=(k2 == 0), stop=(k2 == 1)),
                                 reads=["wsmall", "cn"], writes=[pak])
                        for k2 in range(2):
                            S.op("pe", lambda: nc.tensor.matmul(pb[0:96, :T], wuqs[:, h, k2, :], cn[:, k2, :T],
                                                                start=(k2 == 0), stop=(k2 == 1)),
                                 reads=["wsmall", "cn"], writes=[pbk])
                        sg, sgk = nextstg()
                        S.op("act", lambda: nc.scalar.copy(out=sg[0:64, :T], in_=pa[0:64, :T]), reads=[pak], writes=[sgk])
                        rope_epi(pa, pak, pb, pbk, 64, 96, T, tb, tbk, sg, sgk)
                        S.dma("sp", s["QM"][h, :, c0:c0 + T], sg[0:96, :T], reads=[sgk])
                for (cz, cs, dst, r0) in [(CH_DQ[i], CH_DQS[i], "DQ", i * 128) for i in range(4)] + \
                                         [(CH_SQ[i], CH_SQS[i], "SQ", i * 128) for i in range(4)]:
                    wz, wzk = ws.get(cz)
                    wsw, wswk = ws.get(cs)
                    for blk in blocks:
                        c0, T = blk[0], blk[1]
                        pa, pak = nextps()
                        fm_group(pa, pak, wz, wzk, 128, blk)
                        pb, pbk = nextps()
                        fm_group(pb, pbk, wsw, wswk, 128, blk)
                        tb, tbk = load_tab("rope64", c0, T)
                        sg, sgk = nextstg()
                        rope_epi(pa, pak, pb, pbk, 0, 128, T, tb, tbk, sg, sgk)
                        S.dma("sp", s[dst][r0:r0 + 128, c0:c0 + T], sg[:, :T], reads=[sgk])
                for gi in range(24):
                    wz, wzk = ws.get(CH_G0 + gi)
                    for blk in blocks:
                        c0, T = blk[0], blk[1]
                        pa, pak = nextps()
                        fm_group(pa, pak, wz, wzk, 128, blk)
                        sg, sgk = nextstg()
                        S.op("act", lambda: nc.scalar.activation(out=sg[:, :T], in_=pa[:, :T], func=AF.Sigmoid),
                             reads=[pak], writes=[sgk])
                        S.dma("sp", s["GT"][gi * 128:(gi + 1) * 128, c0:c0 + T], sg[:, :T], reads=[sgk])

            blocksA = [(0, NCTX, 1, 0)] + [(NCTX + 512 * i, 512, 0, i + 1) for i in range(4)]
            for (c0, T, s_, bi) in blocksA:
                self.emit_norm_h(tl, lambda kc: self.X[:, kc, c0:c0 + T], ["X"], T,
                                 lambda kc: hT[:, kc, c0:c0 + T], [("hT", bi)], s_, 0)
            ws = WStream(KV_SEQ + Q_SEQ)
            kv_pass(ws, blocksA, 0)
            q_pass(ws, [b for b in blocksA if not (self.last and b[2] == 1)])
            S.barrier()
            blocksB = [(512 * i, 512, 0, i) for i in range(4)]
            gvs = [g.rearrange("(r p k) t -> r p k t", r=2, k=2) for g in self.gout]
            for (c0, T, s_, bi) in blocksB:
                if not self.gathered:
                    S.dma("sp", xob[:], d["xo"][:, :, c0:c0 + T], writes=["xob"])
                else:
                    for kc in range(KC):
                        ta, tak = tl["t"][tl["ti"] % 4], ("t", tl["ti"] % 4)
                        tl["ti"] += 1
                        tb2, tbk2 = tl["t"][tl["ti"] % 4], ("t", tl["ti"] % 4)
                        tl["ti"] += 1
                        gv = gvs[kc // 2]
                        S.dma("sp", ta[:, :T], gv[0, :, kc % 2, c0:c0 + T], reads=[("gout", kc // 2)], writes=[tak])
                        S.dma("sp", tb2[:, :T], gv[1, :, kc % 2, c0:c0 + T], reads=[("gout", kc // 2)], writes=[tbk2])
                        S.op("dve", lambda: nc.vector.tensor_scalar(out=ta[:, :T], in0=ta[:, :T], scalar1=self.sel[:, 0:1],
                                                                    scalar2=None, op0=ALU.mult),
                             reads=[tak, "consts"], writes=[tak])
                        S.op("dve", lambda: nc.vector.scalar_tensor_tensor(out=xob[:, kc, :T], in0=tb2[:, :T],
                                                                           scalar=self.sel[:, 1:2], in1=ta[:, :T],
                                                                           op0=ALU.mult, op1=ALU.add),
                             reads=[tbk2, tak, "consts"], writes=["xob"])
                self.emit_norm_h(tl, lambda kc: xob[:, kc, :T], ["xob"], T,
                                 lambda kc: hT[:, kc, c0:c0 + T], [("hT", bi)], 0, 0)
            ws = WStream(KV_SEQ)
            kv_pass(ws, blocksB, TA)

    def phase_attn(self, l):
        nc, S, d, s = self.nc, self.S, self.d, self.s
        lam_init = 0.8 - 0.6 * math.exp(-0.3 * l)
        qbl = self.qblocks()
        with ExitStack() as ph:
            KT = [self.sb(ph, "KT", [128, NK], BF16) for _ in range(2)]
            QT = [self.sb(ph, "QT", [128, TA], BF16) for _ in range(2)]
            QZ = [[self.sb(ph, "QZ", [128, TA], BF16) for _ in range(2)] for _ in range(2)]
            PT = [self.sb(ph, "PT", [128, 512], BF16) for _ in range(6)]
            rec = self.sb(ph, "rec", [128, 512], F32)
            bcsb = [self.sb(ph, "bcsb", [128, 512], F32) for _ in range(2)]
            ub = [self.sb(ph, "ub", [128, 512], F32) for _ in range(6)]
            sqb = [self.sb(ph, "sqb", [128, 512], BF16) for _ in range(2)]
            ostg = [self.sb(ph, "ostg", [128, 512], BF16) for _ in range(4)]
            sm = self.small
            dl = self.sb(ph, "dl", [128, 256], F32)
            dnw = self.sb(ph, "dnw", [64, 2], F32)
            sk8 = self.sb(ph, "sk8", [128, 8], F32)
            sinkrow = self.sb(ph, "sinkrow", [128, 2, 512], F32)
            maskb = self.sb(ph, "maskb", [128, 4, 512], BF16)
            cnt = {"pt": 0, "ostg": 0, "sb": 0}
            phv = ExitStack()
            VA = self.sb(phv, "VA", [128, NKT, 520], BF16)

            S.dma("sp", dl[64:65, :], d["dlam"][l], writes=["dl"])
            S.dma("sp", dnw[:], d["dnorm"][l], writes=["dnw"])
            S.dma("sp", sk8[64:65, :], d["sink"][l], writes=["sk8"])
            S.dma("pool", maskb[:], d["maskb"], writes=["maskb"])
            for b_ in range(2):
                for e_ in range(2):
                    S.op("pool", lambda: nc.gpsimd.memset(QZ[b_][e_][:], 0.0), writes=[("QZ", b_, e_)])
            R = slice(64, 65)
            S.op("dve", lambda: nc.vector.tensor_tensor(out=sm[R, 0:64], in0=dl[R, 0:64], in1=dl[R, 64:128], op=ALU.mult),
                 reads=["dl"], writes=["sm_a"])
            S.op("dve", lambda: nc.vector.tensor_tensor(out=sm[R, 64:128], in0=dl[R, 128:192], in1=dl[R, 192:256], op=ALU.mult),
                 reads=["dl"], writes=["sm_b"])
            S.op("dve", lambda: nc.vector.reduce_sum(out=sm[R, 128:129], in_=sm[R, 0:64], axis=AX.X),
                 reads=["sm_a"], writes=["sm_c"])
            S.op("dve", lambda: nc.vector.reduce_sum(out=sm[R, 129:130], in_=sm[R, 64:128], axis=AX.X),
                 reads=["sm_b"], writes=["sm_d"])
            S.op("act", lambda: nc.scalar.activation(out=sm[R, 130:132], in_=sm[R, 128:130], func=AF.Exp),
                 reads=["sm_c", "sm_d"], writes=["sm_e"])
            S.op("dve", lambda: nc.vector.scalar_tensor_tensor(out=sm[R, 132:133], in0=sm[R, 131:132], scalar=-lam_init,
                                                               in1=sm[R, 130:131], op0=ALU.add, op1=ALU.subtract),
                 reads=["sm_e"], writes=["neglam"])
            S.op("dve", lambda: nc.vector.tensor_scalar(out=dnw[:], in0=dnw[:], scalar1=(1.0 - lam_init), scalar2=None,
                                                        op0=ALU.mult), reads=["dnw"], writes=["dnw"])
            S.op("act", lambda: nc.scalar.activation(out=sk8[R, :], in_=sk8[R, :], func=AF.Exp),
                 reads=["sk8"], writes=["sk8"])
            for g in range(2):
                for i in range(4):
                    S.op("dve", lambda: nc.vector.tensor_scalar(out=sinkrow[R, g, i * 128:(i + 1) * 128],
                                                                in0=self.zrow[R, 0:128],
                                                                scalar1=sk8[R, g * 4 + i:g * 4 + i + 1], scalar2=None,
                                                                op0=ALU.add),
                         reads=["sk8", "consts"], writes=["sinkrow"])

            pending = [None]

            def attn_block(n, T, qk_fn, scale, pv_fn, fin_fn):
                sb_list = sbank_cfg["banks"]
                LA = sbank_cfg["la"]
                sbanks = []

                def qk(i):
                    bi = sb_list[cnt["sb"] % len(sb_list)]
                    cnt["sb"] += 1
                    qk_fn(i, self.ps[bi], ("ps", bi))
                    sbanks.append(bi)
                for i in range(min(LA, n)):
                    qk(i)
                for i in range(n):
                    if i + LA < n:
                        qk(i + LA)
                    bi = sbanks[i]
                    pi = cnt["pt"] % len(PT)
                    cnt["pt"] += 1
                    S.op("act", lambda: nc.scalar.activation(out=PT[pi][:, :T], in_=self.ps[bi][:, :T], func=AF.Exp, scale=scale),
                         reads=[("ps", bi)], writes=[("pt", pi)])
                    pv_fn(i, PT[pi], ("pt", pi))
                    if pending[0] is not None and i == min(3, n - 1):
                        pending[0]()
                        pending[0] = None
                if pending[0] is not None:
                    pending[0]()
                pending[0] = fin_fn

            def flush():
                if pending[0] is not None:
                    pending[0]()
                    pending[0] = None

            def next_ostg():
                i = cnt["ostg"] % len(ostg)
                cnt["ostg"] += 1
                return ostg[i], ("ostg", i)

            sbank_cfg = {"banks": [0, 1, 2, 5, 7], "la": 2}

            def next_sbank():
                return 3

            def bcast_recip(T, zsrc_fn, zkeys, scale_neglam=False):
                zsrc_fn()
                S.op("dve", lambda: nc.vector.reciprocal(out=rec[R, :T], in_=rec[R, :T]), reads=["rec"], writes=["rec"])
                if scale_neglam:
                    S.op("dve", lambda: nc.vector.tensor_scalar(out=rec[R, :T], in0=rec[R, :T], scalar1=sm[R, 132:133],
                                                                scalar2=None, op0=ALU.mult),
                         reads=["rec", "neglam"], writes=["rec"])
                bb = next_sbank()
                S.op("pe", lambda: nc.tensor.matmul(self.ps[bb][:, :T], self.onesf[R, 0:128], rec[R, :T], start=True, stop=True),
                     reads=["rec", "consts"], writes=[("ps", bb)])
                S.op("act", lambda: nc.scalar.copy(out=bcsb[0][:, :T], in_=self.ps[bb][:, :T]),
                     reads=[("ps", bb)], writes=[("bcsb", 0)])

            S.dma("sp", VA[:, 0:17, :], s["VM"][:, 0:17, :], writes=["VA"])
            S.dma("sp", VA[:, 17:34, :], s["VM"][:, 17:34, :], writes=["VA"])

            def load_mla(h):
                b = h % 2
                S.dma("sp", KT[b][0:64, :], s["KN"][h * 64:(h + 1) * 64, :], writes=[("KT", b)])
                S.dma("sp", KT[b][64:96, :], s["KR"][:, :], writes=[("KT", b)])
                S.dma("sp", QT[b][0:96, :], s["QM"][h], writes=[("QT", b)])
            load_mla(0)
            accsel = [0]
            for h in range(8):
                if h + 1 < 8:
                    load_mla(h + 1)
                b = h % 2
                for (c0, T, s_) in qbl:
                    ktiles = [0, 1] if s_ == 1 else list(range(NKT))
                    n = len(ktiles)
                    ai = 4 + 2 * (accsel[0] % 2)
                    accsel[0] += 1
                    acc, acck = self.ps[ai], ("ps", ai)

                    def qk_fn(i, psb, psk, ktiles=ktiles, c0=c0, T=T, b=b):
                        kt = ktiles[i]
                        S.op("pe", lambda: nc.tensor.matmul(psb[:, :T], KT[b][0:96, kt * 128:(kt + 1) * 128],
                                                            QT[b][0:96, c0:c0 + T], start=True, stop=True),
                             reads=[("KT", b), ("QT", b)], writes=[psk])

                    def pv_fn(i, pt, ptk, ktiles=ktiles, T=T, h=h, acc=acc, acck=acck, n=n):
                        kt = ktiles[i]
                        S.op("pe", lambda: nc.tensor.matmul(acc[0:65, :T], VA[:, kt, h * 65:(h + 1) * 65], pt[:, :T],
                                                            start=(i == 0), stop=(i == n - 1)),
                             reads=[ptk, "VA"], writes=[acck])

                    def fin(c0=c0, T=T, h=h, acc=acc, acck=acck):
                        bcast_recip(T, lambda: S.op("dve", lambda: nc.vector.tensor_copy(out=rec[R, :T], in_=acc[R, :T]),
                                                    reads=[acck], writes=["rec"]), [acck])
                        og, ogk = next_ostg()
                        S.op("dve", lambda: nc.vector.tensor_tensor(out=og[0:64, :T], in0=acc[0:64, :T], in1=bcsb[0][0:64, :T],
                                                                    op=ALU.mult),
                             reads=[acck, ("bcsb", 0)], writes=[ogk])
                        S.dma("pool", s["OA"][h * 64:(h + 1) * 64, c0:c0 + T], og[0:64, :T], reads=[ogk])
                    attn_block(n, T, qk_fn, MLA_SCALE, pv_fn, fin)
            flush()
            if self.stop == "mla":
                S.barrier()
                phv.close()
                return

            sbank_cfg["banks"] = [0, 1, 2]
            sbank_cfg["la"] = 2
            S.dma("sp", VA[:, 0:17, :], s["DV"][:, 0:17, :], writes=["VA"])
            S.dma("sp", VA[:, 17:34, :], s["DV"][:, 17:34, :], writes=["VA"])

            def load_diff(j):
                b = j % 2
                S.dma("sp", KT[b][:, :], s["DK"][j * 128:(j + 1) * 128, :], writes=[("KT", b)])
                for e in range(2):
                    S.dma("sp", QZ[b][e][e * 64:(e + 1) * 64, :], s["DQ"][j * 128 + e * 64:j * 128 + (e + 1) * 64, :],
                          writes=[("QZ", b, e)])
            load_diff(0)
            for j in range(4):
                if j + 1 < 4:
                    load_diff(j + 1)
                b = j % 2
                for (c0, T, s_) in qbl:
                    ktiles = [0, 1] if s_ == 1 else list(range(NKT))
                    n = len(ktiles)
                    for e in range(2):
                        ai = 4 + 2 * (accsel[0] % 2)
                        accsel[0] += 1
                        acca, accak = self.ps[ai], ("ps", ai)
                        accb, accbk = self.ps[ai + 1], ("ps", ai + 1)

                        def qk_fn(i, psb, psk, ktiles=ktiles, c0=c0, T=T, b=b, e=e):
                            kt = ktiles[i]
                            S.op("pe", lambda: nc.tensor.matmul(psb[:, :T], KT[b][:, kt * 128:(kt + 1) * 128],
                                                                QZ[b][e][:, c0:c0 + T], start=True, stop=True),
                                 reads=[("KT", b), ("QZ", b, e)], writes=[psk])

                        def pv_fn(i, pt, ptk, ktiles=ktiles, T=T, j=j, acca=acca, accak=accak, accb=accb, accbk=accbk, n=n):
                            kt = ktiles[i]
                            S.op("pe", lambda: nc.tensor.matmul(acca[0:65, :T], VA[:, kt, (2 * j) * 65:(2 * j + 1) * 65], pt[:, :T],
                                                                start=(i == 0), stop=(i == n - 1)),
                                 reads=[ptk, "VA"], writes=[accak])
                            S.op("pe", lambda: nc.tensor.matmul(accb[0:65, :T], VA[:, kt, (2 * j + 1) * 65:(2 * j + 2) * 65], pt[:, :T],
                                                                start=(i == 0), stop=(i == n - 1)),
                                 reads=[ptk, "VA"], writes=[accbk])

                        def fin(c0=c0, T=T, j=j, e=e, acca=acca, accak=accak, accb=accb, accbk=accbk):
                            bcast_recip(T, lambda: S.op("dve", lambda: nc.vector.tensor_copy(out=rec[R, :T], in_=acca[R, :T]),
                                                        reads=[accak], writes=["rec"]), [accak], scale_neglam=(e == 1))
                            S.op("dve", lambda: nc.vector.tensor_tensor(out=ub[2 * e][0:64, :T], in0=acca[0:64, :T],
                                                                        in1=bcsb[0][0:64, :T], op=ALU.mult),
                                 reads=[accak, ("bcsb", 0)], writes=[("ub", 2 * e)])
                            S.op("dve", lambda: nc.vector.tensor_tensor(out=ub[2 * e + 1][0:64, :T], in0=accb[0:64, :T],
                                                                        in1=bcsb[0][0:64, :T], op=ALU.mult),
                                 reads=[accbk, ("bcsb", 0)], writes=[("ub", 2 * e + 1)])
                            if e == 1:
                                for hh in range(2):
                                    S.op("pool", lambda: nc.gpsimd.tensor_tensor(out=ub[4 + hh][0:64, :T], in0=ub[hh][0:64, :T],
                                                                                 in1=ub[2 + hh][0:64, :T], op=ALU.add),
                                         reads=[("ub", hh), ("ub", 2 + hh)], writes=[("ub", 4 + hh)])
                                    S.op("act", lambda: nc.scalar.activation(out=sqb[hh][0:64, :T], in_=ub[4 + hh][0:64, :T],
                                                                             func=AF.Square),
                                         reads=[("ub", 4 + hh)], writes=[("sqb", hh)])
                                mb = next_sbank()
                                for hh in range(2):
                                    S.op("pe", lambda: nc.tensor.matmul(self.ps[mb][:, :T], self.ones128[0:64, :], sqb[hh][0:64, :T],
                                                                        start=(hh == 0), stop=(hh == 1)),
                                         reads=[("sqb", hh), "consts"], writes=[("ps", mb)])
                                S.op("act", lambda: nc.scalar.activation(out=bcsb[1][0:64, :T], in_=self.ps[mb][0:64, :T], func=AF.Ln, bias=EPS),
                                     reads=[("ps", mb)], writes=[("bcsb", 1)])
                                S.op("act", lambda: nc.scalar.activation(out=bcsb[1][0:64, :T], in_=bcsb[1][0:64, :T], func=AF.Exp, scale=-0.5),
                                     reads=[("bcsb", 1)], writes=[("bcsb", 1)])
                                for hh in range(2):
                                    og, ogk = next_ostg()
                                    S.op("dve", lambda: nc.vector.scalar_tensor_tensor(
                                        out=og[0:64, :T], in0=ub[4 + hh][0:64, :T], scalar=dnw[:, hh:hh + 1], in1=bcsb[1][0:64, :T],
                                        op0=ALU.mult, op1=ALU.mult), reads=[("ub", 4 + hh), ("bcsb", 1), "dnw"], writes=[ogk])
                                    S.dma("pool", s["OB"][j * 128 + hh * 64:j * 128 + (hh + 1) * 64, c0:c0 + T], og[0:64, :T],
                                          reads=[ogk])
                        attn_block(n, T, qk_fn, DH_SCALE, pv_fn, fin)
            flush()

            S.barrier()
            phv.close()
            if self.stop == "diff":
                return
            sbank_cfg["banks"] = [0, 1, 2, 5, 7]
            sbank_cfg["la"] = 2
            SVt = self.sb(ph, "SVt", [128, NKT, 130], BF16)
            S.dma("sp", SVt[:, :, :], s["SV"][:, :, :], writes=["SVt"])
            QG = [self.sb(ph, "QG", [128, 4, TA], BF16) for _ in range(2)]
            S.dma("sp", KT[0][:, :], s["SK"][:, :], writes=[("KT", 0)])
            for g in range(2):
                S.op("pool", lambda: nc.gpsimd.memset(QG[g][:], 0.0), writes=[("QG", g)])
                S.dma("sp", QG[g][g * 64:(g + 1) * 64, :, :],
                      s["SQ"][g * 256:(g + 1) * 256, :].rearrange("(i e) t -> e i t", e=64), writes=[("QG", g)])
            for g in range(2):
                qtiles = []
                if not self.last:
                    qtiles += [(0, None), (128, None)]
                qtiles += [(NCTX + 128 * qi, qi) for qi in range(16)]
                for (col0, qi) in qtiles:
                    if qi is None:
                        tiles = [(0, None), (1, None)]
                    else:
                        kt = 2 + qi
                        prev = (kt - 1, 1) if qi >= 1 else (33, 0)
                        nxt = (kt + 1, 2) if qi <= 14 else (18, 3)
                        tiles = [(0, None), (1, None), prev, (kt, None), nxt]
                    n = len(tiles)
                    ai = 4 + 2 * (accsel[0] % 2)
                    accsel[0] += 1
                    acc, acck = self.ps[ai], ("ps", ai)

                    def qk_fn(i, psb, psk, tiles=tiles, col0=col0, g=g):
                        kt, mi = tiles[i]
                        S.op("pe", lambda: nc.tensor.matmul(psb[:, :], KT[0][:, kt * 128:(kt + 1) * 128],
                                                            QG[g][:, :, col0:col0 + 128], start=True, stop=(mi is None)),
                             reads=[("KT", 0), ("QG", g)], writes=[psk])
                        if mi is not None:
                            S.op("pe", lambda: nc.tensor.matmul(psb[:, :], self.ident[:], maskb[:, mi, :], start=False, stop=True),
                                 reads=["consts", "maskb"], writes=[psk])

                    def pv_fn(i, pt, ptk, tiles=tiles, g=g, acc=acc, acck=acck, n=n):
                        kt, mi = tiles[i]
                        S.op("pe", lambda: nc.tensor.matmul(acc[0:65, :], SVt[:, kt, g * 65:(g + 1) * 65], pt[:, :],
                                                            start=(i == 0), stop=(i == n - 1)),
                             reads=[ptk, "SVt"], writes=[acck])

                    def fin(col0=col0, g=g, acc=acc, acck=acck):
                        bcast_recip(512, lambda: S.op("dve", lambda: nc.vector.tensor_tensor(
                            out=rec[R, :], in0=acc[R, :], in1=sinkrow[R, g, :], op=ALU.add),
                            reads=[acck, "sinkrow"], writes=["rec"]), [acck])
                        og, ogk = next_ostg()
                        S.op("dve", lambda: nc.vector.tensor_tensor(out=og[0:64, :], in0=acc[0:64, :], in1=bcsb[0][0:64, :],
                                                                    op=ALU.mult),
                             reads=[acck, ("bcsb", 0)], writes=[ogk])
                        S.dma("pool", s["OC"][g * 256:(g + 1) * 256, col0:col0 + 128].rearrange("(i e) q -> e i q", e=64),
                              og[0:64, :].rearrange("e (i q) -> e i q", q=128), reads=[ogk])
                    attn_block(n, 512, qk_fn, DH_SCALE, pv_fn, fin)
                flush()

    def phase_merge(self, l):
        nc, S, d, s = self.nc, self.S, self.d, self.s
        with ExitStack() as ph:
            wbr = self.sb(ph, "wbr", [128, 3, 4, 1024], BF16)
            wo = self.sb(ph, "wo", [128, KC, 1024], BF16)
            ot = [self.sb(ph, "ot", [128, 12, 512], BF16) for _ in range(2)]
            gt = [self.sb(ph, "gt", [128, 3, 512], BF16) for _ in range(3)]
            mt = self.sb(ph, "mt", [128, KC, 512], BF16)
            yf = self.sb(ph, "yf", [128, KC, 512], F32)
            tm = [self.sb(ph, "tm", [128, 512], F32) for _ in range(6)]
            sq = [self.sb(ph, "sq", [128, 512], BF16) for _ in range(2)]
            rstd = self.sb(ph, "rstd", [128, 512], F32)
            for br in range(3):
                S.dma("pool", wbr[:, br, :, :], d["w_br"][l, br], writes=["wbr"])
            S.dma("pool", wo[:], d["w_o"][l], writes=["wo"])
            onames = ("OA", "OB", "OC")
            cnt = {"gt": 0, "tm": 0, "sq": 0, "ps": 0}
            qbl = self.qblocks()

            def load_o(bi):
                c0, T, s_ = qbl[bi]
                o = ot[bi % 2]
                for br in range(3):
                    S.dma("sp", o[:, br * 4:(br + 1) * 4, :T],
                          s[onames[br]][:, c0:c0 + T].rearrange("(kc p) t -> p kc t", p=128), writes=[("ot", bi % 2)])
            load_o(0)
            for bi, (c0, T, s_) in enumerate(qbl):
                if bi + 1 < len(qbl):
                    load_o(bi + 1)
                o, ok = ot[bi % 2], ("ot", bi % 2)
                gview = s["GT"][:, c0:c0 + T].rearrange("(br oc p) t -> oc p br t", br=3, oc=8)
                for oc in range(KC):
                    gi = cnt["gt"] % 3
                    cnt["gt"] += 1
                    S.dma("sp", gt[gi][:, :, :T], gview[oc], writes=[("gt", gi)])
                    tms = []
                    for br in range(3):
                        pi = cnt["ps"] % 5
                        cnt["ps"] += 1
                        psb, psk = self.ps[pi], ("ps", pi)
                        for kc in range(4):
                            S.op("pe", lambda: nc.tensor.matmul(psb[:, :T], wbr[:, br, kc, oc * 128:(oc + 1) * 128],
                                                                o[:, br * 4 + kc, :T], start=(kc == 0), stop=(kc == 3)),
                                 reads=["wbr", ok], writes=[psk])
                        ti = cnt["tm"] % 6
                        cnt["tm"] += 1
                        S.op("dve", lambda: nc.vector.tensor_tensor(out=tm[ti][:, :T], in0=psb[:, :T], in1=gt[gi][:, br, :T],
                                                                    op=ALU.mult),
                             reads=[psk, ("gt", gi)], writes=[("tm", ti)])
                        tms.append(ti)
                    S.op("pool", lambda: nc.gpsimd.tensor_tensor(out=tm[tms[0]][:, :T], in0=tm[tms[0]][:, :T], in1=tm[tms[1]][:, :T],
                                                                 op=ALU.add),
                         reads=[("tm", tms[0]), ("tm", tms[1])], writes=[("tm", tms[0])])
                    S.op("pool", lambda: nc.gpsimd.tensor_tensor(out=mt[:, oc, :T], in0=tm[tms[0]][:, :T], in1=tm[tms[2]][:, :T],
                                                                 op=ALU.add),
                         reads=[("tm", tms[0]), ("tm", tms[2])], writes=[("mt", oc)])
                pst = self.ps[7]
                pend_stat = None
                for oc in range(KC):
                    pi = cnt["ps"] % 5
                    cnt["ps"] += 1
                    psb, psk = self.ps[pi], ("ps", pi)
                    for kc in range(KC):
                        S.op("pe", lambda: nc.tensor.matmul(psb[:, :T], wo[:, kc, oc * 128:(oc + 1) * 128], mt[:, kc, :T],
                                                            start=(kc == 0), stop=(kc == KC - 1)),
                             reads=["wo"] + [("mt", k) for k in range(KC)], writes=[psk])
                    if pend_stat is not None:
                        pend_stat()
                    S.op("act", lambda: nc.scalar.copy(out=yf[:, oc, :T], in_=psb[:, :T]), reads=[psk], writes=[("yf", oc)])
                    si = cnt["sq"] % 2
                    cnt["sq"] += 1
                    S.op("act", lambda: nc.scalar.activation(out=sq[si][:, :T], in_=psb[:, :T], func=AF.Square),
                         reads=[psk], writes=[("sq", si)])

                    def pend_stat(oc=oc, si=si, T=T):
                        S.op("pe", lambda: nc.tensor.matmul(pst[:, :T], self.onesD[:], sq[si][:, :T],
                                                            start=(oc == 0), stop=(oc == KC - 1)),
                             reads=[("sq", si), "consts"], writes=["ps7"])
                pend_stat()
                self.post_norm_residual(pst, rstd, yf, [("yf", k) for k in range(KC)], lambda oc: yf[:, oc, :T],
                                        c0, T, s_, 2, tm, cnt)

    def post_norm_residual(self, pst, rstd, yf, yfkeys, ysrc, c0, T, s_, gj, tm, cnt):
        nc, S = self.nc, self.S
        S.op("act", lambda: nc.scalar.activation(out=rstd[:, :T], in_=pst[:, :T], func=AF.Sqrt, bias=EPS, scale=1.0),
             reads=["ps7"], writes=["rstd"])
        S.op("dve", lambda: nc.vector.reciprocal(out=rstd[:, :T], in_=rstd[:, :T]), reads=["rstd"], writes=["rstd"])
        for oc in range(KC):
            ti = cnt["tm"] % len(tm)
            cnt["tm"] += 1
            S.op("dve", lambda: nc.vector.tensor_tensor(out=tm[ti][:, :T], in0=ysrc(oc), in1=rstd[:, :T], op=ALU.mult),
                 reads=[yfkeys[oc], "rstd"], writes=[("tm", ti)])
            S.op("dve", lambda: nc.vector.scalar_tensor_tensor(
                out=self.X[:, oc, c0:c0 + T], in0=tm[ti][:, :T], scalar=self.DER[:, s_, gj, oc:oc + 1],
                in1=self.X[:, oc, c0:c0 + T], op0=ALU.mult, op1=ALU.add),
                reads=[("tm", ti), "DER", "X"], writes=["X"])

    def phase_ffn(self, l):
        nc, S, d = self.nc, self.S, self.d
        if self.last:
            passes = [[(NCTX, 512, 0), (NCTX + 512, 512, 0)], [(NCTX + 1024, 512, 0), (NCTX + 1536, 512, 0)]]
        else:
            passes = [[(0, 256, 1), (256, 512, 0), (768, 384, 0)], [(1152, 512, 0), (1664, 512, 0), (2176, 128, 0)]]
        for blocks in passes:
            Tp = sum(b[1] for b in blocks)
            offs = []
            o = 0
            for b in blocks:
                offs.append(o)
                o += b[1]
            with ExitStack() as ph:
                a = self.sb(ph, "a", [128, NFC, Tp], BF16)
                with ExitStack() as ph1:
                    hf = self.sb(ph1, "hf", [128, KC, Tp], BF16)
                    tl = {"sq": [self.sb(ph1, "sq", [128, 512], BF16) for _ in range(2)], "sqi": 0,
                          "t": [self.sb(ph1, "t", [128, 512], F32) for _ in range(4)], "ti": 0,
                          "rstd": self.sb(ph1, "rstd", [128, 512], F32)}
                    wgu = [self.sb(ph1, "wgu", [128, KC, 256], BF16) for _ in range(3)]
                    sg = [self.sb(ph1, "sg", [128, 512], F32) for _ in range(3)]
                    for bi, (c0, T, s_) in enumerate(blocks):
                        of = offs[bi]
                        self.emit_norm_h(tl, lambda kc: self.X[:, kc, c0:c0 + T], ["X"], T,
                                         lambda kc: hf[:, kc, of:of + T], [("hf", bi)], s_, 3)
                    cnt = {"ps": 0, "sg": 0}
                    for fc in range(NFC):
                        w, wk = wgu[fc % 3], ("wgu", fc % 3)
                        S.dma("pool", w[:], d["w_gu"][l, fc], writes=[wk])
                        for bi, (c0, T, s_) in enumerate(blocks):
                            of = offs[bi]
                            pg, pgk = self.ps[cnt["ps"] % 6], ("ps", cnt["ps"] % 6)
                            cnt["ps"] += 1
                            pu, puk = self.ps[cnt["ps"] % 6], ("ps", cnt["ps"] % 6)
                            cnt["ps"] += 1
                            for kc in range(KC):
                                S.op("pe", lambda: nc.tensor.matmul(pg[:, :T], w[:, kc, 0:128], hf[:, kc, of:of + T],
                                                                    start=(kc == 0), stop=(kc == KC - 1)),
                                     reads=[wk, ("hf", bi)], writes=[pgk])
                            for kc in range(KC):
                                S.op("pe", lambda: nc.tensor.matmul(pu[:, :T], w[:, kc, 128:256], hf[:, kc, of:of + T],
                                                                    start=(kc == 0), stop=(kc == KC - 1)),
                                     reads=[wk, ("hf", bi)], writes=[puk])
                            si = cnt["sg"] % 3
                            cnt["sg"] += 1
                            S.op("act", lambda: nc.scalar.activation(out=sg[si][:, :T], in_=pg[:, :T], func=AF.Silu),
                                 reads=[pgk], writes=[("sg", si)])
                            S.op("dve", lambda: nc.vector.tensor_tensor(out=a[:, fc, of:of + T], in0=pu[:, :T], in1=sg[si][:, :T],
                                                                        op=ALU.mult),
                                 reads=[puk, ("sg", si)], writes=[("a", bi)])
                    S.barrier()
                if self.stop == "ffn_gu":
                    return
                with ExitStack() as ph2:
                    yf = self.sb(ph2, "yf", [128, KC, Tp], F32)
                    wdn = [self.sb(ph2, "wdn", [128, NFC, 128], BF16) for _ in range(2)]
                    sq = [self.sb(ph2, "sq", [128, 512], BF16) for _ in range(2)]
                    tm = [self.sb(ph2, "tm", [128, 512], F32) for _ in range(4)]
                    rstd = self.sb(ph2, "rstd", [128, 512], F32)
                    cnt = {"ps": 0, "tm": 0, "sq": 0}
                    for oc in range(KC):
                        w, wk = wdn[oc % 2], ("wdn", oc % 2)
                        S.dma("pool", w[:], d["w_dn"][l, oc], writes=[wk])
                        for bi, (c0, T, s_) in enumerate(blocks):
                            of = offs[bi]
                            psb, psk = self.ps[cnt["ps"] % 6], ("ps", cnt["ps"] % 6)
                            cnt["ps"] += 1
                            for kc in range(NFC):
                                S.op("pe", lambda: nc.tensor.matmul(psb[:, :T], w[:, kc, :], a[:, kc, of:of + T],
                                                                    start=(kc == 0), stop=(kc == NFC - 1)),
                                     reads=[wk, ("a", bi)], writes=[psk])
                            S.op("act", lambda: nc.scalar.copy(out=yf[:, oc, of:of + T], in_=psb[:, :T]),
                                 reads=[psk], writes=[("yf", bi, oc)])
                    pst = self.ps[7]
                    for bi, (c0, T, s_) in enumerate(blocks):
                        of = offs[bi]
                        for oc in range(KC):
                            si = cnt["sq"] % 2
                            cnt["sq"] += 1
                            S.op("act", lambda: nc.scalar.activation(out=sq[si][:, :T], in_=yf[:, oc, of:of + T], func=AF.Square),
                                 reads=[("yf", bi, oc)], writes=[("sq", si)])
                            S.op("pe", lambda: nc.tensor.matmul(pst[:, :T], self.onesD[:], sq[si][:, :T],
                                                                start=(oc == 0), stop=(oc == KC - 1)),
                                 reads=[("sq", si), "consts"], writes=["ps7"])
                        self.post_norm_residual(pst, rstd, yf, [("yf", bi, k) for k in range(KC)],
                                                lambda oc: yf[:, oc, of:of + T], c0, T, s_, 5, tm, cnt)
                    S.barrier()
                if self.stop == "ffn_p1":
                    return


_CACHE = {}


def _get_prog(layers, dbg=False):
    key = (tuple(layers), dbg)
    if key not in _CACHE:
        p = Prog(list(layers), dbg)
        p.build()
        _CACHE[key] = p
    return _CACHE[key]


def _core_inputs(x, ctx, c, c_ctx, wts, consts):
    maps = []
    for r in range(8):
        b, half = r // 2, r % 2
        own = x[b, half * SOWN:(half + 1) * SOWN]
        oth = x[b, (1 - half) * SOWN:(2 - half) * SOWN]
        m = {"xa": _fm(np.concatenate([ctx[b], own], axis=0)),
             "xo": _fm(oth),
             "cT": _fm(np.stack([c[b], c_ctx], axis=0))}
        m.update(wts)
        m.update(consts[half])
        m["sel"] = np.ascontiguousarray(np.tile(np.array([[1.0, 0.0]] if half == 1 else [[0.0, 1.0]], np.float32), (128, 1)))
        maps.append(m)
    return maps


def _gather(results, x, ctx):
    x = x.copy()
    ctx = ctx.copy()
    for r in range(8):
        b, half = r // 2, r % 2
        xo = results[r]["xout"]
        full = xo.transpose(2, 1, 0).reshape(TA, D)
        x[b, half * SOWN:(half + 1) * SOWN] = full[NCTX:]
        if half == 0:
            ctx[b] = full[:NCTX]
    return x, ctx


def kernel(**inputs):
    x = np.asarray(inputs["x"], np.float32)
    c = np.asarray(inputs["c"], np.float32)
    ctx = np.asarray(inputs["ctx"], np.float32)
    c_ctx = np.asarray(inputs["c_ctx"], np.float32)
    wts = _prep_weights(inputs)
    consts = [_prep_core_consts(0), _prep_core_consts(1)]
    prog = _get_prog(list(range(DEPTH)))
    maps = _core_inputs(x, ctx, c, c_ctx, wts, consts)
    res = run_bass_kernel_spmd(prog.nc, maps, core_ids=list(range(8)))
    x, ctx = _gather(res.results, x, ctx)
    return x
```
